# Optimizing a Trainium2 kernel written in Bass

```python
import jax, jax.numpy as jnp
from jax import lax
import numpy as np

D_MODEL = 1024
BATCH = 2
SEQ = 8192
DEPTH = 1

CHUNK = 64
Q_BLOCK = 128
SB_HEAD_DIM = 64
SB_HEADS = (D_MODEL // 2) // SB_HEAD_DIM
SB_WIDTH = SB_HEADS * SB_HEAD_DIM
GLA_HEADS = 4
GLA_DV = (D_MODEL // 2) // GLA_HEADS
GLA_DK = GLA_DV // 2
GLA_VWIDTH = GLA_HEADS * GLA_DV
GLA_KWIDTH = GLA_HEADS * GLA_DK
GLA_GATE_RANK = 16
GLA_TAU = 16.0
MIX_WIDTH = SB_WIDTH + GLA_VWIDTH
D_FF = 2816
CONV_WIDTH = 3
N_MOD = 6
EPS = 1e-6
IN_SPLITS = (SB_WIDTH, SB_WIDTH, SB_WIDTH,
             GLA_KWIDTH, GLA_KWIDTH, GLA_VWIDTH,
             GLA_VWIDTH, GLA_GATE_RANK)
IN_WIDTH = sum(IN_SPLITS)

kernel_name = "hybrid_stickbreak_gla_convffn_adaln"


def rms_norm(x, gain):
    xf = x.astype(jnp.float32)
    n = xf * lax.rsqrt(jnp.mean(xf * xf, axis=-1, keepdims=True) + EPS)
    return (n * gain.astype(jnp.float32)).astype(x.dtype)


def modulate(h, shift, scale):
    return h * (1 + scale[:, None, :]) + shift[:, None, :]


def stick_breaking_attention(q, k, v):
    B, S, H, d = q.shape
    q = q.transpose(0, 2, 1, 3)
    k = k.transpose(0, 2, 1, 3)
    v = v.transpose(0, 2, 1, 3)
    scale = d ** -0.5
    outs = []
    for i in range(S // Q_BLOCK):
        start = i * Q_BLOCK
        end = start + Q_BLOCK
        qb = q[:, :, start:end]
        kb = k[:, :, :end]
        vb = v[:, :, :end]
        z = jnp.einsum('bhqd,bhkd->bhqk', qb, kb).astype(jnp.float32) * scale
        t_pos = start + jnp.arange(Q_BLOCK)
        s_pos = jnp.arange(end)
        mask = s_pos[None, :] < t_pos[:, None]
        log_one_minus = jnp.where(mask, jax.nn.log_sigmoid(-z), 0.0)
        shifted = jnp.concatenate(
            [log_one_minus[..., 1:], jnp.zeros_like(log_one_minus[..., :1])], axis=-1)
        between = lax.cumsum(shifted, axis=3, reverse=True)
        weights = jnp.where(mask, jnp.exp(jax.nn.log_sigmoid(z) + between), 0.0)
        outs.append(jnp.einsum('bhqk,bhkd->bhqd', weights.astype(vb.dtype), vb))
    o = jnp.concatenate(outs, axis=2)
    return o.transpose(0, 2, 1, 3).reshape(B, S, H * d)


def gla_chunk_causal(q, k, v, log_alpha):
    B, S, H, dk = q.shape
    dv = v.shape[-1]
    nc = S // CHUNK
    qc = q.reshape(B, nc, CHUNK, H, dk).astype(jnp.float32) * (dk ** -0.5)
    kc = k.reshape(B, nc, CHUNK, H, dk).astype(jnp.float32)
    vc = v.reshape(B, nc, CHUNK, H, dv).astype(jnp.float32)
    la = log_alpha.reshape(B, nc, CHUNK, H, dk).astype(jnp.float32)
    cum = jnp.cumsum(la, axis=2)
    total = cum[:, :, -1]
    k_dec = kc * jnp.exp(total[:, :, None] - cum)
    chunk_kv = jnp.einsum('bnchk,bnchv->nbhkv', k_dec, vc)
    chunk_decay = jnp.exp(total).transpose(1, 0, 2, 3)

    def step(state, inp):
        decay, kv = inp
        state = decay[..., None] * state + kv
        return state, state

    s0 = jnp.zeros((B, H, dk, dv), jnp.float32)
    _, states = lax.scan(step, s0, (chunk_decay, chunk_kv))
    o = jnp.einsum('bnchk,nbhkv->bnchv', qc, states)
    return o.reshape(B, S, H * dv)


def causal_depthwise_conv(u, w, b):
    S = u.shape[1]
    up = jnp.pad(u, ((0, 0), (CONV_WIDTH - 1, 0), (0, 0)))
    out = b
    for j in range(CONV_WIDTH):
        out = out + w[j] * up[:, j:j + S]
    return out


def setup_inputs(seed: int = 0) -> dict:
    key = jax.random.key(seed)
    ks = jax.random.split(key, 18)

    def nrm(k, shape, scale):
        return jax.random.normal(k, shape, jnp.float32) * scale

    L, D = DEPTH, D_MODEL
    return {
        "x": nrm(ks[0], (BATCH, SEQ, D), 1.0),
        "c": nrm(ks[1], (BATCH, D), 1.0),
        "w_ada": nrm(ks[2], (L, D, N_MOD * D), 0.5 * D ** -0.5),
        "b_ada": nrm(ks[3], (L, N_MOD * D), 0.01),
        "g_norm1": 1.0 + nrm(ks[4], (L, D), 0.02),
        "w_in": nrm(ks[5], (L, D, IN_WIDTH), D ** -0.5),
        "w_fg2": nrm(ks[6], (L, GLA_GATE_RANK, GLA_KWIDTH), GLA_GATE_RANK ** -0.5),
        "b_fg2": nrm(ks[7], (L, GLA_KWIDTH), 0.1),
        "g_gla_out": 1.0 + nrm(ks[8], (L, GLA_VWIDTH), 0.02),
        "w_out": nrm(ks[9], (L, MIX_WIDTH, D), MIX_WIDTH ** -0.5),
        "g_norm2": 1.0 + nrm(ks[10], (L, D), 0.02),
        "w_up": nrm(ks[11], (L, D, 2 * D_FF), D ** -0.5),
        "w_conv": nrm(ks[12], (L, CONV_WIDTH, 2 * D_FF), CONV_WIDTH ** -0.5),
        "b_conv": nrm(ks[13], (L, 2 * D_FF), 0.01),
        "w_down": nrm(ks[14], (L, D_FF, D), D_FF ** -0.5),
        "g_final": 1.0 + nrm(ks[15], (D,), 0.02),
    }


def reference(x, c, w_ada, b_ada, g_norm1, w_in, w_fg2, b_fg2, g_gla_out, w_out,
              g_norm2, w_up, w_conv, b_conv, w_down, g_final):
    B, S, _ = x.shape
    offsets = np.cumsum(IN_SPLITS)[:-1].tolist()
    for l in range(DEPTH):
        mod = jax.nn.silu(c) @ w_ada[l] + b_ada[l]
        shift1, scale1, gate1, shift2, scale2, gate2 = jnp.split(mod, N_MOD, axis=-1)

        h = modulate(rms_norm(x, g_norm1[l]), shift1, scale1)
        proj = h @ w_in[l]
        sb_q, sb_k, sb_v, gq, gk, gv, gg, gf = jnp.split(proj, offsets, axis=-1)

        o_sb = stick_breaking_attention(
            sb_q.reshape(B, S, SB_HEADS, SB_HEAD_DIM),
            sb_k.reshape(B, S, SB_HEADS, SB_HEAD_DIM),
            sb_v.reshape(B, S, SB_HEADS, SB_HEAD_DIM))

        log_alpha = jax.nn.log_sigmoid(
            (gf @ w_fg2[l] + b_fg2[l]).astype(jnp.float32)) / GLA_TAU
        o_gla = gla_chunk_causal(
            gq.reshape(B, S, GLA_HEADS, GLA_DK),
            gk.reshape(B, S, GLA_HEADS, GLA_DK),
            gv.reshape(B, S, GLA_HEADS, GLA_DV),
            log_alpha.reshape(B, S, GLA_HEADS, GLA_DK))
        oh = o_gla.reshape(B, S, GLA_HEADS, GLA_DV)
        oh = oh * lax.rsqrt(jnp.mean(oh * oh, axis=-1, keepdims=True) + EPS)
        o_gla = (oh.reshape(B, S, GLA_VWIDTH) * g_gla_out[l].astype(jnp.float32)
                 ).astype(x.dtype) * jax.nn.silu(gg)

        mixed = jnp.concatenate([o_sb.astype(x.dtype), o_gla], axis=-1) @ w_out[l]
        x = x + (1 + gate1[:, None, :]) * mixed

        h2 = modulate(rms_norm(x, g_norm2[l]), shift2, scale2)
        u = causal_depthwise_conv(h2 @ w_up[l], w_conv[l], b_conv[l])
        val, gte = jnp.split(u, 2, axis=-1)
        x = x + (1 + gate2[:, None, :]) * ((val * jax.nn.silu(gte)) @ w_down[l])
    return rms_norm(x, g_final)
```

```python
import os
import numpy as np
from collections import deque
DBGMASK = int(os.environ.get('DBGMASK', '0'))
PAIRDEPTH = int(os.environ.get('PAIRDEPTH', '7'))
LAZY = int(os.environ.get('LAZY', '1'))
LAZY_FROM = int(os.environ.get('LAZY_FROM', '6'))
PSPLIT = int(os.environ.get('PSPLIT', '2'))
from contextlib import ExitStack
import concourse.bass as bass
import concourse.mybir as mybir
from concourse.bass_utils import run_bass_kernel_spmd

F32 = mybir.dt.float32
BF16 = mybir.dt.bfloat16
I32 = mybir.dt.int32
U8 = mybir.dt.uint8
AF = mybir.ActivationFunctionType
ALU = mybir.AluOpType

SEQ = 8192
DM = 1024
NT = 16
NB = 64
DFF = 2816
NCH = 44
EPS = 1e-6
AGW = SEQ + 2
TB = 410

C_C = 0
C_BADA = 8
C_G1 = 56
C_G2 = 64
C_GF = 72
C_WC = 80
C_BC = 212
C_GGLA = 256
C_HALO = 257
C_WFG = 258
C_BFG = 322
C_CONST = 386
C_MSUF = C_CONST + 8 * 128
C_CSEL = C_MSUF + 128
C_ONESF = C_CSEL + 2
PW = C_ONESF + 128
K_TRIA, K_TRIB, K_MASK, K_SHA, K_SHB, K_DMA, K_DMB, K_ONES = range(8)

ENGS = ("pe", "act", "dve", "pool", "sp")


class Sched:
    def __init__(self):
        self.ops = {e: [] for e in ENGS}
        self.lastw = {}
        self.readers = {}
        self.slotcnt = {}

    def _deps(self, eng, r, w, extra):
        deps = set(t for t in extra if t is not None)
        for k in r:
            if k in self.lastw:
                deps.add(self.lastw[k])
        for k in w:
            if k in self.lastw:
                deps.add(self.lastw[k])
            deps |= self.readers.get(k, set())
        if eng == "pe":
            deps = set(d for d in deps if not (d[0] == "e" and d[1] == "pe"))
        return deps

    def _commit(self, tok, r, w):
        for k in r:
            self.readers.setdefault(k, set()).add(tok)
        for k in w:
            self.lastw[k] = tok
            self.readers[k] = set()

    def op(self, eng, fn, r=(), w=(), extra=()):
        deps = self._deps(eng, r, w, extra)
        idx = len(self.ops[eng])
        self.ops[eng].append(dict(fn=fn, deps=deps, dma=None))
        tok = ("e", eng, idx)
        self._commit(tok, r, w)
        return tok

    def barrier(self, eng, w=()):
        extra = []
        for e in ENGS:
            for i in range(len(self.ops[e]) - 1, -1, -1):
                if self.ops[e][i]["dma"] is None:
                    extra.append(("e", e, i))
                    break
        for s_, c_ in self.slotcnt.items():
            if s_.startswith("cc"):
                continue
            extra.append(("d", s_, c_))
        nop = (lambda h: h.engine_nop()) if eng == "pool" else (lambda h: h.nop())
        return self.op(eng, nop, w=w, extra=extra)

    def dma(self, eng, slot, fn, r=(), w=(), extra=(), inc=16):
        deps = self._deps(eng, r, w, extra)
        cnt = self.slotcnt.get(slot, 0) + (inc if inc else 1)
        self.slotcnt[slot] = cnt
        self.ops[eng].append(dict(fn=fn, deps=deps, dma=slot, inc=inc))
        tok = ("d", slot, cnt)
        self._commit(tok, r, w)
        return tok

    def emit(self, nc, handles):
        refd = {e: set() for e in ENGS}
        for e in ENGS:
            for o in self.ops[e]:
                for d in o["deps"]:
                    if d[0] == "e":
                        refd[d[1]].add(d[2])
        cnt = {}
        for e in ENGS:
            c = 0
            for i, o in enumerate(self.ops[e]):
                if i in refd[e]:
                    c += 1
                    cnt[(e, i)] = c
        with ExitStack() as st:
            esem = {e: st.enter_context(nc.semaphore("sem_" + e)) for e in ENGS}
            dsem = {s: st.enter_context(nc.semaphore("dsem_%d" % i)) for i, s in enumerate(self.slotcnt)}
            block = st.enter_context(nc.Block())

            def run(e, h):
                seen = {}
                for i, o in enumerate(self.ops[e]):
                    waits = {}
                    for d in o["deps"]:
                        if d[0] == "e":
                            key, val = ("e", d[1]), cnt[(d[1], d[2])]
                        else:
                            key, val = ("d", d[1]), d[2]
                        if seen.get(key, 0) >= val:
                            continue
                        waits[key] = max(waits.get(key, 0), val)
                    for key, val in waits.items():
                        seen[key] = val
                        h.wait_ge(esem[key[1]] if key[0] == "e" else dsem[key[1]], val)
                    inst = o["fn"](h)
                    if o["dma"] is not None:
                        if o["inc"]:
                            inst.then_inc(dsem[o["dma"]], o["inc"])
                        else:
                            inst.then_inc(dsem[o["dma"]])
                    elif i in refd[e]:
                        inst.then_inc(esem[e], 1)
                if e == "sp":
                    for s, c in self.slotcnt.items():
                        h.wait_ge(dsem[s], c)

            @block.tensor
            def _(h):
                run("pe", h)

            @block.scalar
            def _(h):
                run("act", h)

            @block.vector
            def _(h):
                run("dve", h)

            @block.gpsimd
            def _(h):
                run("pool", h)

            @block.sync
            def _(h):
                run("sp", h)


class Arena:
    def __init__(self, ap, nbytes):
        self.ap = ap
        self.n = nbytes
        self.off = 0

    def alloc(self, shape, dtype, parts=128):
        esz = {F32: 4, BF16: 2, I32: 4}[dtype]
        n = int(np.prod(shape))
        nb = (n * esz + 31) // 32 * 32
        assert self.off + nb <= self.n, ("arena overflow", self.off, nb, self.n)
        v = self.ap[0:parts, self.off:self.off + nb][:, 0:n * esz].bitcast(dtype)
        self.off += nb
        if len(shape) == 2:
            v = v.rearrange("p (a b) -> p a b", b=shape[1])
        elif len(shape) == 3:
            v = v.rearrange("p (a b c) -> p a b c", b=shape[1], c=shape[2])
        return v


def build_program(debug=False, stop=None, nsteps=None):
    nc = bass.Bass("TRN2", target_bir_lowering=False)
    xT = nc.dram_tensor("xT", [DM, SEQ], F32, kind="ExternalInput").ap()
    xB = nc.dram_tensor("xB", [DM, 2050], F32, kind="ExternalInput").ap()
    prm_d = nc.dram_tensor("prm", [128, PW], F32, kind="ExternalInput").ap()
    wada_d = nc.dram_tensor("wada", [12, 128, 8, 512], F32, kind="ExternalInput").ap()
    win_d = nc.dram_tensor("win", [128, 8, 784], F32, kind="ExternalInput").ap()
    wout_d = nc.dram_tensor("wout", [128, 8, 1024], F32, kind="ExternalInput").ap()
    wup_d = nc.dram_tensor("wup", [22, 128, 2, 8, 128], F32, kind="ExternalInput").ap()
    wdn_d = nc.dram_tensor("wdn", [128, 22, 1024], F32, kind="ExternalInput").ap()
    idx_d = nc.dram_tensor("tokoff", [1, 2], I32, kind="ExternalInput").ap()
    yT = nc.dram_tensor("yT", [DM, 2048], F32, kind="ExternalOutput").ap()
    agin = [nc.dram_tensor("agin%d" % k, [256, 2048], BF16) for k in range(4)]
    agout = nc.dram_tensor("agout", [4 * 1024, 2048], BF16)
    if debug:
        dbg_o = nc.dram_tensor("dbg_o", [256, SEQ], F32, kind="ExternalOutput").ap()

    S = Sched()
    st = ExitStack()
    sb = lambda name, shape, dt: st.enter_context(nc.sbuf_tensor(name, shape, dt))
    ps = lambda name, shape, dt=F32: st.enter_context(nc.psum_tensor(name, shape, dt))

    prm = sb("prm_sb", [128, PW], F32)
    cb = sb("cb", [128, 8 * 128], BF16)
    modsb = sb("modsb", [128, 48], F32)
    a1 = sb("a1", [128, 8], F32)
    a2 = sb("a2", [128, 8], F32)
    g1p = sb("g1p", [128, 8], F32)
    g2p = sb("g2p", [128, 8], F32)
    scb = sb("scb", [128, 8], BF16)
    sctmp = sb("sctmp", [128, 16], F32)
    zero2 = sb("zero2", [128, 2, 2], BF16)
    ARENA_BYTES = 198 * 1024
    arena_t = sb("arena", [128, ARENA_BYTES], U8)
    reg_off = st.enter_context(nc.gpsimd.register("tokoff_reg"))
    reg_off2 = st.enter_context(nc.gpsimd.register("tokoff_reg2"))

    bank2 = [ps("bank2_%d" % i, [128, 1024]) for i in range(4)]
    banks = [bank2[i // 2][:, 512 * (i % 2):512 * (i % 2) + 512] for i in range(8)]
    ZZ = bank2[0][:, :].rearrange("p (h c) -> p h c", h=2)
    Z = [banks[0], banks[1]]
    CB = [banks[2], banks[3]]
    OUTB = banks[4]
    PA, PB, PC = banks[5], banks[6], banks[7]

    def cm(k):
        return cb[:, 128 * k:128 * k + 128]

    shift1 = modsb[:, 0:8]
    shift2 = modsb[:, 24:32]

    S.dma("sp", "prm", lambda h: h.dma_start(out=prm[:], in_=prm_d[:, :]), w=["prm"])
    S.op("dve", lambda h: h.tensor_copy(out=cb[:], in_=prm[:, C_CONST:C_CONST + 1024]), r=["prm"], w=["cb"])
    S.op("dve", lambda h: h.memset(zero2[:], 0.0), w=["zero2"])
    S.op("act", lambda h: h.activation(out=sctmp[:, 0:8], in_=prm[:, C_C:C_C + 8], func=AF.Exp, scale=-1.0),
         r=["prm"], w=["sct0"])
    S.op("dve", lambda h: h.tensor_scalar(out=sctmp[:, 8:16], in0=sctmp[:, 0:8], scalar1=1.0, scalar2=None,
                                          op0=ALU.add), r=["sct0"], w=["sct1"])
    S.op("dve", lambda h: h.reciprocal(out=sctmp[:, 0:8], in_=sctmp[:, 8:16]), r=["sct1"], w=["sct0"])
    S.op("dve", lambda h: h.tensor_tensor(out=scb[:], in0=sctmp[:, 0:8], in1=prm[:, C_C:C_C + 8], op=ALU.mult),
         r=["sct0", "prm"], w=["scb"])

    A = Arena(arena_t, ARENA_BYTES)
    qT = A.alloc([SEQ], BF16)
    kT = A.alloc([SEQ], BF16)
    Vtok = A.alloc([NB, 128], BF16)
    dvtok = A.alloc([NB, 128], BF16)
    win = A.alloc([8, 784], BF16)
    wada = [A.alloc([8, 512], BF16) for _ in range(4)]
    NX = 8
    xs = [A.alloc([512], F32) for _ in range(NX)]
    sq = A.alloc([8, 512], BF16)
    hT = A.alloc([8, 512], BF16)
    tmpb = [A.alloc([512], F32) for _ in range(2)]
    lnv = A.alloc([512], F32)
    rstd = A.alloc([512], F32)
    ebm = A.alloc([2, 512], F32)
    eb = [ebm[:, 0, :], ebm[:, 1, :]]
    spm = [A.alloc([2, 512], BF16) for _ in range(2)]
    spb = [[spm[0][:, hd_, :], spm[1][:, hd_, :]] for hd_ in range(2)]
    Eb = [[A.alloc([512], BF16) for _ in range(2)] for _ in range(2)]
    osb = [A.alloc([512], BF16) for _ in range(2)]
    ogl = [A.alloc([512], BF16) for _ in range(2)]
    gqT2 = [A.alloc([512], BF16) for _ in range(2)]
    sg2 = [A.alloc([512], F32) for _ in range(2)]
    gtmp = A.alloc([512], F32)
    gfT2 = [A.alloc([512], F32) for _ in range(2)]
    gvb2 = [[A.alloc([128], BF16) for _ in range(4)] for _ in range(2)]
    gkb2 = [[A.alloc([64], F32) for _ in range(4)] for _ in range(2)]
    ey2 = [A.alloc([64], F32) for _ in range(2)]
    spy2 = [A.alloc([64], F32) for _ in range(2)]
    Dd2 = [A.alloc([64], F32) for _ in range(2)]
    dec2 = [A.alloc([2], F32) for _ in range(2)]
    kdec2 = [A.alloc([64], BF16) for _ in range(2)]
    Sst = [A.alloc([128], F32) for _ in range(2)]
    Sb = [A.alloc([128], BF16) for _ in range(8)]
    sqo = A.alloc([512], BF16)
    lno = A.alloc([512], F32)
    rso = A.alloc([512], F32)
    ot1 = A.alloc([512], F32)
    if debug:
        dbgf = [A.alloc([512], F32) for _ in range(2)]
    print("phase A arena bytes:", A.off)

    def mod_dma(m):
        buf = wada[m % 4]
        key = "wada%d" % (m % 4)
        S.dma("pool", key, lambda h: h.dma_start(out=buf, in_=wada_d[m], max_dma_last_dim=2048), w=[key])

    def mod_compute(m):
        buf = wada[m % 4]
        key = "wada%d" % (m % 4)
        for j in range(4):
            col = 4 * m + j
            for kc in range(8):
                S.op("pe", lambda h, j=j, kc=kc, col=col: h.matmul(
                    PC[:, col:col + 1], lhsT=buf[:, kc, 128 * j:128 * j + 128], rhs=scb[:, kc:kc + 1],
                    start=(kc == 0), stop=(kc == 7)), r=[key, "scb"], w=["PC"])
        S.op("dve", lambda h: h.tensor_tensor(out=modsb[:, 4 * m:4 * m + 4], in0=PC[:, 4 * m:4 * m + 4],
                                              in1=prm[:, C_BADA + 4 * m:C_BADA + 4 * m + 4], op=ALU.add),
             r=["PC", "prm"], w=["mod%d" % m])

    def mod_front():
        for m in range(4):
            mod_compute(m)
        S.op("dve", lambda h: h.scalar_tensor_tensor(out=a1[:], in0=modsb[:, 8:16], scalar=1.0,
                                                     in1=prm[:, C_G1:C_G1 + 8], op0=ALU.add, op1=ALU.mult),
             r=["mod2", "mod3", "prm"], w=["a1"])

    def mod_back_tasks():
        T_ = []
        for m in range(4, 12):
            T_.append((lambda m=m: mod_dma(m), True))
            T_.append((lambda m=m: mod_compute(m), True))

        def fin():
            S.op("dve", lambda h: h.tensor_scalar(out=g1p[:], in0=modsb[:, 16:24], scalar1=1.0, scalar2=None,
                                                  op0=ALU.add), r=["mod4", "mod5"], w=["g1p"])
            S.op("dve", lambda h: h.scalar_tensor_tensor(out=a2[:], in0=modsb[:, 32:40], scalar=1.0,
                                                         in1=prm[:, C_G2:C_G2 + 8], op0=ALU.add, op1=ALU.mult),
                 r=["mod8", "mod9", "prm"], w=["a2"])
            S.op("dve", lambda h: h.tensor_scalar(out=g2p[:], in0=modsb[:, 40:48], scalar1=1.0, scalar2=None,
                                                  op0=ALU.add), r=["mod10", "mod11"], w=["g2p"])
        T_.append((fin, True))
        return T_

    for m in range(4):
        mod_dma(m)

    def load_regs(h):
        h.reg_load(reg_off, idx_d[0:1, 0:1])
        return h.reg_load(reg_off2, idx_d[0:1, 1:2])
    S.op("pool", load_regs)
    S.dma("pool", "win", lambda h: h.dma_start(out=win, in_=win_d[:, :, :], max_dma_last_dim=3136), w=["win"])

    def finish():
        S.emit(nc, None)
        st.close()
        return nc
    if stop == "setup":
        return finish()
    xcount = [0]

    def proj_tasks(j):
        C, G = [], []
        t0 = 512 * j
        qd = j % 2
        gqT, sg, gfT = gqT2[qd], sg2[qd], gfT2[qd]
        kgq, ksg, kgf = "gqT%d" % qd, "sg%d" % qd, "gfT%d" % qd
        xslots = []

        ess = [True]

        def add(fn, hop=False):
            C.append((fn, hop, ess[0]))

        def addg(fn, hop=False):
            G.append((fn, hop))

        def t_load():
            for kc in range(8):
                sl = xcount[0] % NX
                xcount[0] += 1
                xslots.append(sl)
                S.dma("sp", "xs%d" % sl, lambda h, kc=kc, sl=sl: h.dma_start(
                    out=xs[sl], in_=xT[128 * kc:128 * kc + 128, t0:t0 + 512]), w=["xs%d" % sl])
        add(t_load)

        def t_sq(kc):
            sl = xslots[kc]
            S.op("pool", lambda h: h.tensor_tensor(out=sq[:, kc, :], in0=xs[sl], in1=xs[sl], op=ALU.mult),
                 r=["xs%d" % sl], w=["sq%d" % kc])
        for kc in range(8):
            add(lambda kc=kc: t_sq(kc), kc == 0)

        def add_split(fn_kcs, first_hop):
            per = 8 // PSPLIT
            for p_ in range(PSPLIT):
                kcs = list(range(per * p_, per * p_ + per))
                add(lambda kcs=kcs: fn_kcs(kcs), first_hop if p_ == 0 else True)

        def t_ssq(kcs):
            for kc in kcs:
                S.op("pe", lambda h, kc=kc: h.matmul(PA[:, :], lhsT=cm(K_ONES), rhs=sq[:, kc, :],
                                                     start=(kc == 0), stop=(kc == 7)),
                     r=["sq%d" % kc, "cb"], w=["PA"])
        add_split(t_ssq, True)

        def t_rstd():
            S.op("act", lambda h: h.activation(out=lnv, in_=PA[:, :], func=AF.Ln, scale=1.0 / DM, bias=EPS),
                 r=["PA"], w=["lnv"])
            S.op("act", lambda h: h.activation(out=rstd, in_=lnv, func=AF.Exp, scale=-0.5), r=["lnv"], w=["rstd"])
        add(t_rstd, True)

        def t_h(kc):
            sl = xslots[kc]
            tb_ = tmpb[kc % 2]
            S.op("dve", lambda h: h.scalar_tensor_tensor(out=tb_, in0=xs[sl], scalar=a1[:, kc:kc + 1], in1=rstd,
                                                         op0=ALU.mult, op1=ALU.mult),
                 r=["xs%d" % sl, "rstd", "a1"], w=["tmp%d" % (kc % 2)])
            S.op("pool", lambda h: h.tensor_scalar(out=hT[:, kc, :], in0=tb_, scalar1=1.0,
                                                   scalar2=shift1[:, kc:kc + 1], op0=ALU.mult, op1=ALU.add),
                 r=["tmp%d" % (kc % 2), "mod0", "mod1"], w=["hT%d" % kc])
        for kc in range(8):
            add(lambda kc=kc: t_h(kc), kc == 0)

        def fm_group(bank, bkey, c0, M, first_hop):
            def part(kcs):
                for kc in kcs:
                    S.op("pe", lambda h, kc=kc: h.matmul(bank[0:M, :], lhsT=win[:, kc, c0:c0 + M], rhs=hT[:, kc, :],
                                                         start=(kc == 0), stop=(kc == 7)),
                         r=["hT%d" % kc, "win"], w=[bkey])
            add_split(part, first_hop)

        fm_group(PB, "PB", 0, 128, True)
        add(lambda: S.op("dve", lambda h: h.tensor_copy(out=qT[:, t0:t0 + 512], in_=PB[:, :]), r=["PB"],
                         w=["qT%d" % j]), True)
        fm_group(PA, "PA", 128, 128, False)
        add(lambda: S.op("dve", lambda h: h.tensor_copy(out=kT[:, t0:t0 + 512], in_=PA[:, :]), r=["PA"],
                         w=["kT%d" % j]), True)

        def t_tok(tb, part):
            blk = 4 * j + tb
            bank, bkey = (PA, "PA") if tb % 2 == 0 else (PB, "PB")
            gv_ = gvb2[qd][tb]
            gk_ = gkb2[qd][tb]
            kgv, kgk = "gv%d_%d" % (qd, tb), "gk%d_%d" % (qd, tb)
            dvreg = bank[:, 320:448]
            if isinstance(part, tuple):
                for kc in part[1]:
                    S.op("pe", lambda h, kc=kc: h.matmul(bank[:, 0:320], lhsT=hT[:, kc, 128 * tb:128 * tb + 128],
                                                         rhs=win[:, kc, 464:784], start=(kc == 0), stop=(kc == 7)),
                         r=["hT%d" % kc, "win"], w=[bkey])
            elif part == 1:
                S.op("dve", lambda h: h.tensor_copy(out=Vtok[:, blk, :], in_=bank[:, 0:128]), r=[bkey],
                     w=["V%d" % blk])
                S.op("dve", lambda h: h.tensor_copy(out=gv_, in_=bank[:, 192:320]), r=[bkey], w=[kgv])
                S.op("dve", lambda h: h.tensor_copy(out=gk_, in_=bank[:, 128:192]), r=[bkey], w=[kgk])
            elif part == 11:
                S.op("pe", lambda h: h.matmul(dvreg, lhsT=cm(K_DMA), rhs=Vtok[:, blk, :], start=True,
                                              stop=(blk == 0)), r=["V%d" % blk, "cb"], w=[bkey])
                if blk > 0:
                    S.op("pe", lambda h: h.matmul(dvreg, lhsT=cm(K_DMB), rhs=Vtok[:, blk - 1, :], start=False,
                                                  stop=True), r=["V%d" % (blk - 1), "cb"], w=[bkey])
            elif part == 12:
                S.op("dve", lambda h: h.tensor_copy(out=dvtok[:, blk, :], in_=dvreg), r=[bkey], w=["dv%d" % blk])

        for t2_ in (0, 2):
            add_split(lambda kcs, t=t2_: t_tok(t, (0, kcs)), True)
            add_split(lambda kcs, t=t2_: t_tok(t + 1, (0, kcs)), False)
            for part in (1, 11, 12):
                add(lambda part=part, t=t2_: t_tok(t, part), True)
                add(lambda part=part, t=t2_: t_tok(t + 1, part))

        ess[0] = False
        fm_group(PB, "PB", 256, 64, True)
        add(lambda: S.op("dve", lambda h: h.tensor_scalar(out=gqT[0:64, :], in0=PB[0:64, :], scalar1=0.125,
                                                          scalar2=None, op0=ALU.mult), r=["PB"], w=[kgq]), True)
        fm_group(PA, "PA", 320, 128, False)
        add(lambda: S.op("act", lambda h: h.activation(out=gtmp, in_=PA[:, :], func=AF.Exp, scale=-1.0),
                         r=["PA"], w=["gtmp"]), True)

        def t_gg_dve():
            S.op("dve", lambda h: h.tensor_scalar(out=sg, in0=gtmp, scalar1=1.0, scalar2=None, op0=ALU.add),
                 r=["gtmp"], w=[ksg])
            S.op("dve", lambda h: h.reciprocal(out=gtmp, in_=sg), r=[ksg], w=["gtmp"])
            S.op("dve", lambda h: h.tensor_tensor(out=sg, in0=gtmp, in1=PA[:, :], op=ALU.mult),
                 r=["gtmp", "PA"], w=[ksg])
        add(t_gg_dve, True)
        fm_group(PB, "PB", 448, 16, False)
        add(lambda: S.op("dve", lambda h: h.tensor_copy(out=gfT[0:16, :], in_=PB[0:16, :]), r=["PB"], w=[kgf]),
            True)

        def t_gla(tb, part):
            pp = tb % 2
            gv_ = gvb2[qd][tb]
            gk_ = gkb2[qd][tb]
            kgv, kgk = "gv%d_%d" % (qd, tb), "gk%d_%d" % (qd, tb)
            ey, spy, Dd, dec, kdec = ey2[pp], spy2[pp], Dd2[pp], dec2[pp], kdec2[pp]
            kE, kS, kD, kC, kK = "ey%d" % pp, "spy%d" % pp, "Dd%d" % pp, "dec%d" % pp, "kdec%d" % pp
            yb = 256 + 128 * pp
            yreg = PC[:, yb:yb + 64]
            dreg = PC[:, yb + 64:yb + 128]
            treg = PC[0:64, 128 * pp:128 * pp + 2]
            if part == 2:
                S.op("pe", lambda h: h.matmul(yreg, lhsT=gfT[0:16, 128 * tb:128 * tb + 128],
                                              rhs=prm[0:16, C_WFG:C_WFG + 64], start=True, stop=False),
                     r=[kgf, "prm"], w=["PC"])
                S.op("pe", lambda h: h.matmul(yreg, lhsT=prm[0:1, C_ONESF:C_ONESF + 128],
                                              rhs=prm[0:1, C_BFG:C_BFG + 64], start=False, stop=True),
                     r=["prm"], w=["PC"])
            elif part == 3:
                S.op("act", lambda h: h.activation(out=ey, in_=yreg, func=AF.Exp, scale=-1.0), r=["PC"], w=[kE])
                S.op("act", lambda h: h.activation(out=spy, in_=ey, func=AF.Ln, bias=1.0), r=[kE], w=[kS])
            elif part == 4:
                S.op("pe", lambda h: h.matmul(dreg, lhsT=prm[:, C_MSUF:C_MSUF + 128], rhs=spy,
                                              start=True, stop=True), r=[kS, "prm"], w=["PC"])
                S.op("pe", lambda h: h.matmul(treg, lhsT=spy, rhs=prm[:, C_CSEL:C_CSEL + 2],
                                              start=True, stop=True), r=[kS, "prm"], w=["PC"])
            elif part == 5:
                S.op("act", lambda h: h.activation(out=Dd, in_=dreg, func=AF.Exp), r=["PC"], w=[kD])
                S.op("act", lambda h: h.activation(out=dec[0:64, :], in_=treg, func=AF.Exp), r=["PC"], w=[kC])
            elif part == 6:
                S.op("dve", lambda h: h.tensor_tensor(out=kdec, in0=gk_, in1=Dd, op=ALU.mult),
                     r=[kgk, kD], w=[kK])
            elif part in (7, 9):
                c = (part - 7) // 2
                S.op("pe", lambda h: h.matmul(PC[0:64, 128 * c:128 * c + 128], lhsT=kdec[64 * c:64 * c + 64, :],
                                              rhs=gv_[64 * c:64 * c + 64, :], start=True, stop=True),
                     r=[kK, kgv], w=["PC"])
            elif part in (8, 10):
                c = (part - 8) // 2
                ci = 2 * tb + c
                so, sn = Sst[ci % 2], Sst[(ci + 1) % 2]
                if j == 0 and ci == 0:
                    S.op("dve", lambda h: h.tensor_copy(out=sn[0:64, :], in_=PC[0:64, 128 * c:128 * c + 128]),
                         r=["PC"], w=["S%d" % ((ci + 1) % 2)])
                else:
                    S.op("dve", lambda h: h.scalar_tensor_tensor(
                        out=sn[0:64, :], in0=so[0:64, :], scalar=dec[0:64, c:c + 1], in1=PC[0:64, 128 * c:128 * c + 128],
                        op0=ALU.mult, op1=ALU.add),
                        r=["PC", kC, "S%d" % (ci % 2)], w=["S%d" % ((ci + 1) % 2)])
                S.op("dve", lambda h: h.tensor_copy(out=Sb[ci][0:64, :], in_=sn[0:64, :]),
                     r=["S%d" % ((ci + 1) % 2)], w=["Sb%d" % ci])

        for t2_ in (0, 2):
            for part in range(2, 7):
                addg(lambda part=part, t=t2_: t_gla(t, part), True)
                addg(lambda part=part, t=t2_: t_gla(t + 1, part))
            for tb in (t2_, t2_ + 1):
                for part in (7, 8, 9, 10):
                    addg(lambda tb=tb, part=part: t_gla(tb, part), True)

        def glao_pe(hf):
            for ci in range(4 * hf, 4 * hf + 4):
                cl = ci - 4 * hf
                S.op("pe", lambda h, ci=ci, cl=cl: h.matmul(PC[:, 64 * cl:64 * cl + 64], lhsT=Sb[ci][0:64, :],
                                                            rhs=gqT[0:64, 64 * ci:64 * ci + 64], start=True, stop=True),
                     r=["Sb%d" % ci, kgq], w=["PC"])

        def gla_rs(hf):
            S.op("act", lambda h: h.activation(out=lno[:, 0:256], in_=PC[:, 256:512], func=AF.Ln, scale=1.0 / 128,
                                               bias=EPS), r=["PC"], w=["lno"])
            S.op("act", lambda h: h.activation(out=rso[:, 0:256], in_=lno[:, 0:256], func=AF.Exp, scale=-0.5),
                 r=["lno"], w=["rso"])

        def gla_mid(hf):
            S.op("dve", lambda h: h.tensor_tensor(out=ot1[:, 256 * hf:256 * hf + 256], in0=PC[:, 0:256],
                                                  in1=rso[:, 0:256], op=ALU.mult), r=["PC", "rso"], w=["ot1"])
        for hf in range(2):
            addg(lambda hf=hf: glao_pe(hf), True)
            addg(lambda hf=hf: S.op("act", lambda h: h.activation(out=sqo[:, 0:256], in_=PC[:, 0:256],
                                                                  func=AF.Square), r=["PC"], w=["sqo"]), True)
            addg(lambda hf=hf: S.op("pe", lambda h: h.matmul(PC[:, 256:512], lhsT=cm(K_ONES), rhs=sqo[:, 0:256],
                                                             start=True, stop=True), r=["sqo", "cb"], w=["PC"]), True)
            addg(lambda hf=hf: gla_rs(hf), True)
            addg(lambda hf=hf: gla_mid(hf), True)

        def gla_fin():
            ob = ogl[j % 2]
            S.op("dve", lambda h: h.scalar_tensor_tensor(out=ob, in0=ot1, scalar=prm[:, C_GGLA:C_GGLA + 1], in1=sg,
                                                         op0=ALU.mult, op1=ALU.mult),
                 r=["ot1", ksg, "prm"], w=["ogl%d" % (j % 2)])
            S.dma("sp", "ogl%d" % (j % 2), lambda h: h.dma_start(
                out=agin[j // 4][128:256, 512 * (j % 4):512 * (j % 4) + 512], in_=ob),
                  r=["ogl%d" % (j % 2)], w=["agin_g%d" % j])
            if debug:
                S.op("dve", lambda h: h.tensor_copy(out=dbgf[1], in_=ob), r=["ogl%d" % (j % 2)], w=["dbgf1"])
                S.dma("sp", "dbgf1", lambda h: h.dma_start(out=dbg_o[128:256, t0:t0 + 512], in_=dbgf[1]),
                      r=["dbgf1"], w=["dbgo_g%d" % j])
        addg(gla_fin, True)
        return C, G

    steps = [(i, n) for i in range(NT) for n in range(4 * i + 3, -1, -1)]
    NS = len(steps)

    def c0_of(i, n):
        return 128 * max(n - 4 * i, 0)

    def stage_qk(k):
        i, n = steps[k]
        c0 = c0_of(i, n)
        for hd in range(2):
            S.op("pe", lambda h, hd=hd: h.matmul(
                Z[hd][:, c0:512], lhsT=kT[64 * hd:64 * hd + 64, 128 * n:128 * n + 128],
                rhs=qT[64 * hd:64 * hd + 64, 512 * i + c0:512 * i + 512], start=True, stop=True),
                r=["kT%d" % (n // 4), "qT%d" % i], w=["Z%d" % hd])

    def stage1(k):
        i, n = steps[k]
        c0 = c0_of(i, n)
        b = k % 2
        diag = n >= 4 * i
        S.op("act", lambda h: h.activation(out=ebm[:, :, c0:512], in_=ZZ[:, :, c0:512], func=AF.Exp, scale=0.125),
             r=["Z0", "Z1"], w=["e0", "e1"])
        S.op("act", lambda h: h.activation(out=spm[b][:, :, c0:512], in_=ebm[:, :, c0:512], func=AF.Ln, bias=1.0),
             r=["e0", "e1"], w=["sp0%d" % b, "sp1%d" % b])
        if diag:
            for hd in range(2):
                S.op("pool", lambda h, hd=hd: h.tensor_tensor(out=spb[hd][b][:, c0:c0 + 128],
                                                               in0=spb[hd][b][:, c0:c0 + 128], in1=cm(K_MASK),
                                                               op=ALU.mult),
                     r=["sp%d%d" % (hd, b), "cb"], w=["sp%d%d" % (hd, b)])

    def stage2(k, first_pe=True):
        i, n = steps[k]
        c0 = c0_of(i, n)
        b = k % 2
        diag = n >= 4 * i
        first = (n == 4 * i + 3)
        for hd in range(2):
            S.op("act", lambda h, hd=hd: h.activation(out=Eb[hd][b][:, c0:512], in_=CB[hd][:, c0:512], func=AF.Exp),
                 r=["CB%d" % hd], w=["E%d%d" % (hd, b)])
        if diag:
            for hd in range(2):
                S.op("pool", lambda h, hd=hd: h.tensor_tensor(out=Eb[hd][b][:, c0:c0 + 128],
                                                               in0=Eb[hd][b][:, c0:c0 + 128], in1=cm(K_MASK),
                                                               op=ALU.mult),
                     r=["E%d%d" % (hd, b), "cb"], w=["E%d%d" % (hd, b)])
        for hd in range(2):
            hs = slice(64 * hd, 64 * hd + 64)
            if first:
                for a in range(4):
                    blk = 4 * i + a
                    S.op("pe", lambda h, a=a, blk=blk, hs=hs: h.matmul(
                        OUTB[hs, 128 * a:128 * a + 128], lhsT=Vtok[:, blk, hs], rhs=cm(K_SHA),
                        start=(a == 0), stop=True, skip_group_check=True),
                        r=["V%d" % blk, "cb"], w=["OUT%d" % hd])
                    if blk > 0:
                        S.op("pe", lambda h, a=a, blk=blk, hs=hs: h.matmul(
                            OUTB[hs, 128 * a:128 * a + 1], lhsT=Vtok[:, blk - 1, hs], rhs=cb[:, 128 * K_SHB:128 * K_SHB + 1],
                            start=False, stop=True, skip_group_check=True),
                            r=["V%d" % (blk - 1), "cb"], w=["OUT%d" % hd])
            if n > 0:
                S.op("pe", lambda h, hd=hd: h.matmul(CB[hd][:, c0:512], lhsT=cm(K_TRIB), rhs=spb[hd][b][:, c0:512],
                                                     start=False, stop=True, skip_group_check=True),
                     r=["sp%d%d" % (hd, b), "E%d%d" % (hd, b), "cb"], w=["CB%d" % hd])
            S.op("pe", lambda h, hd=hd, hs=hs: h.matmul(OUTB[hs, c0:512], lhsT=dvtok[:, n, hs], rhs=Eb[hd][b][:, c0:512],
                                                        start=False, stop=True, skip_group_check=True),
                 r=["E%d%d" % (hd, b), "dv%d" % n], w=["OUT%d" % hd])
        if n == 0:
            ob = osb[i % 2]
            S.op("dve", lambda h: h.tensor_copy(out=ob, in_=OUTB[:, :]), r=["OUT0", "OUT1"], w=["osb%d" % (i % 2)])
            S.dma("sp", "osb%d" % (i % 2), lambda h: h.dma_start(
                out=agin[i // 4][0:128, 512 * (i % 4):512 * (i % 4) + 512], in_=ob),
                  r=["osb%d" % (i % 2)], w=["agin_s%d" % i])
            if i % 4 == 3:
                issue_cc(i // 4)
            if False:
                kq = i // 4
                S.dma("pool", "cc%d" % kq, lambda h: h.collective_compute(
                    "AllGather", ALU.bypass, replica_groups=[[0, 1, 2, 3], [4, 5, 6, 7]],
                    ins=[agin[kq].ap().opt()], outs=[agout[1024 * kq:1024 * kq + 1024, :].opt()]),
                    r=["agin_s%d" % t_ for t_ in range(4 * kq, 4 * kq + 4)] + ["agin_g%d" % t_ for t_ in range(4 * kq, 4 * kq + 4)],
                    w=["agout%d" % kq], inc=None)
            if debug:
                S.op("dve", lambda h: h.tensor_copy(out=dbgf[0], in_=OUTB[:, :]), r=["OUT0", "OUT1"], w=["dbgf0"])
                S.dma("sp", "dbgf0", lambda h: h.dma_start(out=dbg_o[0:128, 512 * i:512 * i + 512], in_=dbgf[0]),
                      r=["dbgf0"], w=["dbgo_s%d" % i])

    def issue_cc(kq):
        tl_ = 4 * kq + 3
        while pq and pq[0][0] <= tl_:
            pq.popleft()[1]()
        while gqueue and gqueue[0][0] <= tl_:
            gqueue.popleft()[1]()
        if os.environ.get("NOCC"):
            return
        S.dma("pool", "cc%d" % kq, lambda h: h.collective_compute(
            "AllGather", ALU.bypass, replica_groups=[[0, 1, 2, 3], [4, 5, 6, 7]],
            ins=[agin[kq].ap().opt()], outs=[agout[1024 * kq:1024 * kq + 1024, :].opt()]),
            r=["agin_s%d" % t_ for t_ in range(4 * kq, 4 * kq + 4)] + ["agin_g%d" % t_ for t_ in range(4 * kq, 4 * kq + 4)],
            w=["agout%d" % kq], inc=None)

    def stage_c1(k):
        i, n = steps[k]
        c0 = c0_of(i, n)
        b = k % 2
        first = (n == 4 * i + 3)
        for hd in range(2):
            S.op("pe", lambda h, hd=hd: h.matmul(CB[hd][:, c0:512], lhsT=cm(K_TRIA), rhs=spb[hd][b][:, c0:512],
                                                 start=first, stop=True, skip_group_check=True),
                 r=["sp%d%d" % (hd, b), "cb"], w=["CB%d" % hd])

    bonly = (stop == "Bonly")
    if bonly:
        nsteps = 0
    gqueue = deque()
    if bonly:
        mod_front()
        for fn_, _h in mod_back_tasks():
            fn_()
    else:
        c0_, g0_ = proj_tasks(0)
        npre = 1 + 8 + PSPLIT + 1
        for t, _hop, _e in c0_[:npre]:
            t()
        mod_front()
        for t, _hop, _e in c0_[npre:]:
            t()
        for fn_, hop_ in g0_:
            gqueue.append((0, fn_, hop_))
    if stop == "proj0":
        while gqueue:
            gqueue.popleft()[1]()
        return finish()
    pq = deque()
    if nsteps is not None:
        steps = steps[:nsteps]
        NS = len(steps)
    if not bonly:
        stage_qk(0)
    if not bonly:
        for t_ in range(1, NT):
            c_, g_ = proj_tasks(t_)
            for fn_, hop_, e_ in c_:
                pq.append((t_, fn_, hop_, e_))
            for fn_, hop_ in g_:
                gqueue.append((t_, fn_, hop_))
            if t_ == 7:
                for fn_, hop_ in mod_back_tasks():
                    gqueue.append((t_, fn_, hop_))

    def drain_group(qu):
        if not qu:
            return
        if qu is pq:
            while gqueue and gqueue[0][0] <= qu[0][0] - 2:
                gqueue.popleft()[1]()
        qu.popleft()[1]()
        while qu and not qu[0][2]:
            qu.popleft()[1]()

    def drain_paced(cur_tile, slots_left, background):
        if not pq:
            return
        need = 0
        for idx_, _t in enumerate(pq):
            if _t[0] > cur_tile + 1 or (_t[0] == cur_tile + 1 and not _t[3]):
                break
            if _t[2] or idx_ == 0:
                need += 1
        if need > 0:
            g = -(-need // max(slots_left, 1))
            for _ in range(g):
                drain_group(pq)
        elif background:
            drain_group(pq)

    def drain_gla(cur_tile, steps_left_total, steps_left_tile=None):
        if not gqueue:
            return
        if LAZY and cur_tile >= LAZY_FROM:
            ngr = sum(1 for idx_, _t in enumerate(gqueue) if _t[0] <= cur_tile and (_t[2] or idx_ == 0))
            g = -(-ngr // max(steps_left_tile, 1))
        else:
            ngr = sum(1 for idx_, _t in enumerate(gqueue) if _t[2] or idx_ == 0)
            g = -(-ngr // max(steps_left_total, 1))
        for _ in range(g):
            if gqueue and (not pq or pq[0][0] > gqueue[0][0]):
                drain_group(gqueue)

    def force_drain(tile):
        while pq and (pq[0][0] < tile or (pq[0][0] == tile and pq[0][3])):
            while gqueue and gqueue[0][0] <= pq[0][0] - 2:
                gqueue.popleft()[1]()
            pq.popleft()[1]()

    for k in range(NS + 1 if not bonly else 0):
        if k < NS:
            i, n = steps[k]
            if LAZY and i >= LAZY_FROM:
                drain_paced(i, n + 1, False)
            else:
                drain_paced(i, 2 * n + 2, True)
            stage1(k)
        if k + 1 < NS:
            ni, nn = steps[k + 1]
            if nn == 4 * ni + 3:
                force_drain(ni)
            stage_qk(k + 1)
        if k < NS:
            drain_gla(i, NS - k, n + 1)
        if k >= 1:
            stage2(k - 1)
        if k < NS:
            stage_c1(k)
    while pq:
        pq.popleft()[1]()
    while gqueue:
        gqueue.popleft()[1]()

    if nsteps is not None and not bonly:
        for kq in range(4):
            issue_cc(kq)
    if stop == "A":
        return finish()
    S.barrier("pool", w=["phaseA_done"])

    if stop == "AG":
        return finish()
    B = Arena(arena_t, ARENA_BYTES)
    woutb = B.alloc([8, 1024], BF16)
    wdnb = B.alloc([22, 1024], BF16)
    NWU = 6
    wupb = [B.alloc([2, 8, 128], BF16) for _ in range(NWU)]
    oall = B.alloc([8, 2050], BF16)
    xb = B.alloc([8, TB + 2], F32)
    x1 = B.alloc([8, TB + 2], F32)
    sq2 = B.alloc([8, TB + 2], BF16)
    h2 = B.alloc([8, TB + 2], BF16)
    lnb = B.alloc([TB + 2], F32)
    rsb = B.alloc([TB + 2], F32)
    tmp2 = [B.alloc([TB + 2], F32) for _ in range(2)]
    accv = [B.alloc([TB], F32) for _ in range(2)]
    accg = [B.alloc([TB], F32) for _ in range(2)]
    sgb = [B.alloc([TB], F32) for _ in range(2)]
    ay_raw = B.alloc([22 * TB], BF16)
    aT = ay_raw.rearrange("p (a b) -> p a b", b=TB)
    x2 = xb
    yo = ay_raw[:, 0:8 * TB * 2].bitcast(F32).rearrange("p (a b) -> p a b", b=TB)
    print("phase B arena bytes:", B.off)
    U = [banks[0], banks[1], banks[2], banks[3]]
    MB = [banks[4], banks[5]]
    SSB = banks[6]
    YB = [banks[7], banks[6]]

    pa = ["phaseA_done"]
    S.dma("pool", "wout", lambda h: h.dma_start(out=woutb, in_=wout_d[:, :, :], max_dma_last_dim=4096), r=pa, w=["wout"])

    def load_wdn():
        for jj in range(2):
            S.dma("pool", "wdn%d" % jj, lambda h, jj=jj: h.dma_start(out=wdnb[:, 11 * jj:11 * jj + 11, :],
                                                                     in_=wdn_d[:, 11 * jj:11 * jj + 11, :], max_dma_last_dim=4096),
                  r=pa, w=["wdn%d" % jj])

    agv = agout.ap().rearrange("(c p) w -> p c w", p=128)

    def load_off(h):
        off = h.snap(reg_off)
        return h.dma_start(out=oall[:, :, 2:2050], in_=agv[:, bass.ds(off, 8), :])

    def load_off2(h):
        off2 = h.snap(reg_off2)
        return h.dma_start(out=oall[:, :, 0:2], in_=agv[:, bass.ds(off2, 8), 2046:2048])
    agk = ["agout%d" % kq for kq in range(4)]
    if bonly:
        S.op("pool", lambda h: h.memset(oall, 0.0), r=pa, w=["oall", "oallh"])
    else:
        S.dma("pool", "oall", load_off, r=agk + pa, w=["oall"])
        S.dma("pool", "oallh", load_off2, r=agk + pa, w=["oallh"])

    wucount = [0]
    NTB = 5

    wu_issued = [0]

    def issue_wup():
        gi = wu_issued[0]
        if gi >= 22 * NTB:
            return
        wu_issued[0] += 1
        sl_ = gi % NWU
        jp_ = gi % 22
        S.dma("pool", "wup%d" % sl_, lambda h: h.dma_start(out=wupb[sl_], in_=wup_d[jp_], max_dma_last_dim=4096),
              r=pa, w=["wup%d" % sl_])
    for _ in range(NWU):
        issue_wup()
    load_wdn()

    def tile_body(tl):
        cin0 = TB * tl
        nin = min(TB + 2, 2050 - cin0)
        nout = nin - 2
        xbk = ["xb_%d" % fc for fc in range(8)]
        x1k = ["x1_%d" % fc for fc in range(8)]
        S.dma("sp", "xb", lambda h: h.dma_start(
            out=xb[:, :, 0:nin], in_=xB.rearrange("(c p) w -> p c w", p=128)[:, :, cin0:cin0 + nin]),
            r=pa, w=xbk)

        def mix_fc(fc):
            mb, mk = MB[fc % 2], "MB%d" % (fc % 2)

            def mm(c):
                wrow = (c // 2) if c % 2 == 0 else 4 + (c // 2)
                S.op("pe", lambda h: h.matmul(
                    mb[:, 0:nin], lhsT=woutb[:, wrow, 128 * fc:128 * fc + 128], rhs=oall[:, c, cin0:cin0 + nin],
                    start=(c == 0), stop=(c == 7)), r=["wout", "oall", "oallh"], w=[mk])
            for c in range(8):
                mm(c)
            S.op("dve", lambda h: h.scalar_tensor_tensor(
                out=x1[:, fc, 0:nin], in0=mb[:, 0:nin], scalar=g1p[:, fc:fc + 1], in1=xb[:, fc, 0:nin],
                op0=ALU.mult, op1=ALU.add), r=[mk, "xb_%d" % fc, "g1p"], w=["x1_%d" % fc])
            S.op("pool", lambda h: h.tensor_tensor(out=sq2[:, fc, 0:nin], in0=x1[:, fc, 0:nin],
                                                   in1=x1[:, fc, 0:nin], op=ALU.mult),
                 r=["x1_%d" % fc], w=["sq2_%d" % fc])
        for fc in range(8):
            mix_fc(fc)

        def ssq(n_, fc):
            S.op("pe", lambda h: h.matmul(SSB[:, 0:n_], lhsT=cm(K_ONES), rhs=sq2[:, fc, 0:n_],
                                          start=(fc == 0), stop=(fc == 7)), r=["sq2_%d" % fc, "cb"], w=["SSB"])

        def rstd_of(n_):
            for fc in range(8):
                ssq(n_, fc)
            S.op("act", lambda h: h.activation(out=lnb[:, 0:n_], in_=SSB[:, 0:n_], func=AF.Ln, scale=1.0 / DM, bias=EPS),
                 r=["SSB"], w=["lnb"])
            S.op("act", lambda h: h.activation(out=rsb[:, 0:n_], in_=lnb[:, 0:n_], func=AF.Exp, scale=-0.5),
                 r=["lnb"], w=["rsb"])
        rstd_of(nin)

        def h2_fc(fc):
            t2 = tmp2[fc % 2]
            S.op("dve", lambda h: h.scalar_tensor_tensor(
                out=t2[:, 0:nin], in0=x1[:, fc, 0:nin], scalar=a2[:, fc:fc + 1], in1=rsb[:, 0:nin],
                op0=ALU.mult, op1=ALU.mult), r=["x1_%d" % fc, "rsb", "a2"], w=["tmp2_%d" % (fc % 2)])
            S.op("act", lambda h: h.activation(
                out=h2[:, fc, 0:nin], in_=t2[:, 0:nin], func=AF.Identity, bias=shift2[:, fc:fc + 1], scale=1.0),
                r=["tmp2_%d" % (fc % 2), "mod6", "mod7"], w=["h2_%d" % fc])
        for fc in range(8):
            h2_fc(fc)
        h2k = ["h2_%d" % fc for fc in range(8)]
        if tl == 0:
            S.op("dve", lambda h: h.tensor_scalar(out=h2[:, :, 0:2], in0=h2[:, :, 0:2],
                                                  scalar1=prm[:, C_HALO:C_HALO + 1], scalar2=None, op0=ALU.mult),
                 r=h2k + ["prm"], w=h2k)

        def pair(jp):
            sl = wucount[0] % NWU
            wucount[0] += 1
            wk = "wup%d" % sl
            ub = [U[(2 * jp) % 4], U[(2 * jp + 1) % 4]]
            uk = ["U%d" % ((2 * jp) % 4), "U%d" % ((2 * jp + 1) % 4)]
            accs = [accv[jp % 2], accg[jp % 2]]
            acck = ["accv%d" % (jp % 2), "accg%d" % (jp % 2)]

            def upmm(vg, fc):
                S.op("pe", lambda h: h.matmul(
                    ub[vg][:, 0:nin], lhsT=wupb[sl][:, vg, fc, :], rhs=h2[:, fc, 0:nin],
                    start=(fc == 0), stop=(fc == 7)), r=[wk, "h2_%d" % fc], w=[uk[vg]])
            for vg in range(2):
                for fc in range(8):
                    upmm(vg, fc)
            issue_wup()

            def conv(vg):
                ch = jp + 22 * vg
                wc = lambda t_: prm[:, C_WC + 44 * t_ + ch:C_WC + 44 * t_ + ch + 1]
                S.op("act", lambda h: h.activation(
                    out=accs[vg][:, 0:nout], in_=ub[vg][:, 2:nin], func=AF.Identity, scale=wc(2),
                    bias=prm[:, C_BC + ch:C_BC + ch + 1]), r=[uk[vg], "prm"], w=[acck[vg]])
                S.op("dve", lambda h: h.scalar_tensor_tensor(
                    out=accs[vg][:, 0:nout], in0=ub[vg][:, 1:nin - 1], scalar=wc(1), in1=accs[vg][:, 0:nout],
                    op0=ALU.mult, op1=ALU.add), r=[uk[vg], acck[vg], "prm"], w=[acck[vg]])
                S.op("dve", lambda h: h.scalar_tensor_tensor(
                    out=accs[vg][:, 0:nout], in0=ub[vg][:, 0:nin - 2], scalar=wc(0), in1=accs[vg][:, 0:nout],
                    op0=ALU.mult, op1=ALU.add), r=[uk[vg], acck[vg], "prm"], w=[acck[vg]])
            for vg in range(2):
                conv(vg)
            sgt = sgb[jp % 2]
            S.op("act", lambda h: h.activation(out=sgt[:, 0:nout], in_=accs[1][:, 0:nout], func=AF.Silu),
                 r=[acck[1]], w=["sgb%d" % (jp % 2)])
            S.op("pool", lambda h: h.tensor_tensor(out=aT[:, jp, 0:nout], in0=accs[0][:, 0:nout],
                                                   in1=sgt[:, 0:nout], op=ALU.mult),
                 r=[acck[0], "sgb%d" % (jp % 2)], w=["aT%d" % jp])
        for jp in range(22):
            pair(jp)

        def down_fc(fc):
            ybk, yk = (YB[0], "YB0") if fc % 2 == 0 else (MB[0], "MB0")

            def mm(jp):
                S.op("pe", lambda h: h.matmul(
                    ybk[:, 0:nout], lhsT=wdnb[:, jp, 128 * fc:128 * fc + 128], rhs=aT[:, jp, 0:nout],
                    start=(jp == 0), stop=(jp == 21)), r=["aT%d" % jp, "wdn%d" % (jp // 11)], w=[yk])
            for jp in range(22):
                mm(jp)
            S.op("dve", lambda h: h.scalar_tensor_tensor(
                out=x2[:, fc, 0:nout], in0=ybk[:, 0:nout], scalar=g2p[:, fc:fc + 1], in1=x1[:, fc, 2:nin],
                op0=ALU.mult, op1=ALU.add), r=[yk, "x1_%d" % fc, "g2p"], w=["xb_%d" % fc])
            S.op("pool", lambda h: h.tensor_tensor(out=sq2[:, fc, 0:nout], in0=x2[:, fc, 0:nout],
                                                   in1=x2[:, fc, 0:nout], op=ALU.mult),
                 r=["xb_%d" % fc], w=["sq2_%d" % fc])
        for fc in range(8):
            down_fc(fc)
        rstd_of(nout)

        def fin_fc(fc):
            S.op("dve", lambda h: h.scalar_tensor_tensor(
                out=yo[:, fc, 0:nout], in0=x2[:, fc, 0:nout], scalar=prm[:, C_GF + fc:C_GF + fc + 1], in1=rsb[:, 0:nout],
                op0=ALU.mult, op1=ALU.mult), r=["xb_%d" % fc, "rsb", "prm"], w=aTk)
        aTk = ["aT%d" % jp_ for jp_ in range(22)]
        for fc in range(8):
            fin_fc(fc)
        S.dma("sp", "yo", lambda h: h.dma_start(
            out=yT.rearrange("(c p) w -> p c w", p=128)[:, :, cin0:cin0 + nout], in_=yo[:, :, 0:nout]),
            r=aTk, w=["yT%d" % tl])

    for tl in range(NTB):
        tile_body(tl)

    return finish()


def _consts():
    j = np.arange(128)[:, None]
    s = np.arange(128)[None, :]
    m = np.zeros((8, 128, 128), np.float32)
    m[K_TRIA] = -1.0 * (j >= s)
    m[K_TRIB] = -1.0 * (j < s)
    m[K_MASK] = (j < s)
    m[K_SHA] = (j == s - 1)
    m[K_SHB][127, 0] = 1.0
    m[K_DMA] = (j == s - 1).astype(np.float32) - (j == s).astype(np.float32)
    m[K_DMB][127, 0] = 1.0
    m[K_ONES] = 1.0
    msuf = (-(1.0 / 16.0) * ((j > s) & ((j // 64) == (s // 64)))).astype(np.float32)
    csel = np.zeros((128, 2), np.float32)
    csel[:64, 0] = -1.0 / 16.0
    csel[64:, 1] = -1.0 / 16.0
    return m, msuf, csel


def _pcol(v):
    return np.ascontiguousarray(v.reshape(-1, 128).T)


_NC_CACHE = {}


def kernel(x, c, w_ada, b_ada, g_norm1, w_in, w_fg2, b_fg2, g_gla_out, w_out, g_norm2, w_up, w_conv, b_conv,
           w_down, g_final, _debug=False, _stop=None, _nsteps=None):
    f = lambda a: np.asarray(a, dtype=np.float32)
    x, c, w_ada, b_ada, g_norm1, w_in = f(x), f(c), f(w_ada)[0], f(b_ada)[0], f(g_norm1)[0], f(w_in)[0]
    w_fg2, b_fg2, g_gla_out, w_out = f(w_fg2)[0], f(b_fg2)[0], f(g_gla_out)[0], f(w_out)[0]
    g_norm2, w_up, w_conv, b_conv, w_down, g_final = f(g_norm2)[0], f(w_up)[0], f(w_conv)[0], f(b_conv)[0], f(w_down)[0], f(g_final)
    m8, msuf, csel = _consts()
    key = (bool(_debug), _stop, _nsteps)
    if key not in _NC_CACHE:
        _NC_CACHE[key] = build_program(debug=_debug, stop=_stop, nsteps=_nsteps)
    nc = _NC_CACHE[key]

    wada_l = np.ascontiguousarray(w_ada.reshape(8, 128, 12, 512).transpose(2, 1, 0, 3))
    wout_l = np.ascontiguousarray(w_out.reshape(8, 128, 1024).transpose(1, 0, 2))
    wdn_l = np.ascontiguousarray(w_down.reshape(22, 128, 1024).transpose(1, 0, 2))
    wu = w_up.reshape(8, 128, 2, 22, 128)
    wup_l = np.ascontiguousarray(wu.transpose(3, 1, 2, 0, 4))
    xTs = [np.ascontiguousarray(x[b].T) for b in range(2)]
    in_maps = []
    for core in range(8):
        b, g = core // 4, core % 4
        cols = np.concatenate([
            np.arange(128 * g, 128 * g + 128),
            512 + np.arange(128 * g, 128 * g + 128),
            1536 + np.arange(64 * g, 64 * g + 64),
            2560 + np.arange(128 * g, 128 * g + 128),
            3072 + np.arange(16),
            1024 + np.arange(128 * g, 128 * g + 128),
            1792 + np.arange(64 * g, 64 * g + 64),
            2048 + np.arange(128 * g, 128 * g + 128),
        ])
        win_l = np.ascontiguousarray(w_in[:, cols].reshape(8, 128, 784).transpose(1, 0, 2))
        prm = np.zeros((128, PW), np.float32)
        prm[:, C_C:C_C + 8] = _pcol(c[b])
        prm[:, C_BADA:C_BADA + 48] = _pcol(b_ada)
        prm[:, C_G1:C_G1 + 8] = _pcol(g_norm1)
        prm[:, C_G2:C_G2 + 8] = _pcol(g_norm2)
        prm[:, C_GF:C_GF + 8] = _pcol(g_final)
        for t in range(3):
            prm[:, C_WC + 44 * t:C_WC + 44 * t + 44] = _pcol(w_conv[t])
        prm[:, C_BC:C_BC + 44] = _pcol(b_conv)
        prm[:, C_GGLA] = g_gla_out[128 * g:128 * g + 128]
        prm[:, C_HALO] = 0.0 if g == 0 else 1.0
        prm[0:16, C_WFG:C_WFG + 64] = w_fg2[:, 64 * g:64 * g + 64]
        prm[0, C_BFG:C_BFG + 64] = b_fg2[64 * g:64 * g + 64]
        for kk in range(8):
            prm[:, C_CONST + 128 * kk:C_CONST + 128 * kk + 128] = m8[kk]
        prm[:, C_MSUF:C_MSUF + 128] = msuf
        prm[:, C_CSEL:C_CSEL + 2] = csel
        prm[:, C_ONESF:C_ONESF + 128] = 1.0
        xB = np.zeros((DM, 2050), np.float32)
        lo = 2048 * g - 2
        if g == 0:
            xB[:, 2:] = xTs[b][:, 0:2048]
        else:
            xB[:] = xTs[b][:, lo:lo + 2050]
        in_maps.append({
            "xT": xTs[b], "xB": xB, "prm": prm, "wada": wada_l, "win": win_l, "wout": wout_l, "wup": wup_l,
            "wdn": wdn_l, "tokoff": np.array([[8 * g, 8 * max(g - 1, 0)]], np.int32),
        })
    res = run_bass_kernel_spmd(nc, in_maps, core_ids=list(range(8)))
    out = np.empty((2, SEQ, DM), np.float32)
    for core in range(8):
        b, g = core // 4, core % 4
        out[b, 2048 * g:2048 * g + 2048, :] = res.results[core]["yT"].T
    if _debug:
        return out, [res.results[core]["dbg_o"] for core in range(8)]
    return out
```

```python
import os
import numpy as np
from collections import deque
DBGMASK = int(os.environ.get('DBGMASK', '0'))
PAIRDEPTH = int(os.environ.get('PAIRDEPTH', '7'))
LAZY = int(os.environ.get('LAZY', '1'))
LAZY_FROM = int(os.environ.get('LAZY_FROM', '6'))
PSPLIT = int(os.environ.get('PSPLIT', '2'))
from contextlib import ExitStack
import concourse.bass as bass
import concourse.mybir as mybir
from concourse.bass_utils import run_bass_kernel_spmd

F32 = mybir.dt.float32
BF16 = mybir.dt.bfloat16
I32 = mybir.dt.int32
U8 = mybir.dt.uint8
AF = mybir.ActivationFunctionType
ALU = mybir.AluOpType

SEQ = 8192
DM = 1024
NT = 16
NB = 64
DFF = 2816
NCH = 44
EPS = 1e-6
AGW = SEQ + 2
TB = 410

C_C = 0
C_BADA = 8
C_G1 = 56
C_G2 = 64
C_GF = 72
C_WC = 80
C_BC = 212
C_GGLA = 256
C_HALO = 257
C_WFG = 258
C_BFG = 322
C_CONST = 386
C_MSUF = C_CONST + 8 * 128
C_CSEL = C_MSUF + 128
C_ONESF = C_CSEL + 2
PW = C_ONESF + 128
K_TRIA, K_TRIB, K_MASK, K_SHA, K_SHB, K_DMA, K_DMB, K_ONES = range(8)

ENGS = ("pe", "act", "dve", "pool", "sp")


class Sched:
    def __init__(self):
        self.ops = {e: [] for e in ENGS}
        self.lastw = {}
        self.readers = {}
        self.slotcnt = {}

    def _deps(self, eng, r, w, extra):
        deps = set(t for t in extra if t is not None)
        for k in r:
            if k in self.lastw:
                deps.add(self.lastw[k])
        for k in w:
            if k in self.lastw:
                deps.add(self.lastw[k])
            deps |= self.readers.get(k, set())
        if eng == "pe":
            deps = set(d for d in deps if not (d[0] == "e" and d[1] == "pe"))
        return deps

    def _commit(self, tok, r, w):
        for k in r:
            self.readers.setdefault(k, set()).add(tok)
        for k in w:
            self.lastw[k] = tok
            self.readers[k] = set()

    def op(self, eng, fn, r=(), w=(), extra=()):
        deps = self._deps(eng, r, w, extra)
        idx = len(self.ops[eng])
        self.ops[eng].append(dict(fn=fn, deps=deps, dma=None))
        tok = ("e", eng, idx)
        self._commit(tok, r, w)
        return tok

    def barrier(self, eng, w=()):
        extra = []
        for e in ENGS:
            for i in range(len(self.ops[e]) - 1, -1, -1):
                if self.ops[e][i]["dma"] is None:
                    extra.append(("e", e, i))
                    break
        for s_, c_ in self.slotcnt.items():
            if s_.startswith("cc"):
                continue
            extra.append(("d", s_, c_))
        nop = (lambda h: h.engine_nop()) if eng == "pool" else (lambda h: h.nop())
        return self.op(eng, nop, w=w, extra=extra)

    def dma(self, eng, slot, fn, r=(), w=(), extra=(), inc=16):
        deps = self._deps(eng, r, w, extra)
        cnt = self.slotcnt.get(slot, 0) + (inc if inc else 1)
        self.slotcnt[slot] = cnt
        self.ops[eng].append(dict(fn=fn, deps=deps, dma=slot, inc=inc))
        tok = ("d", slot, cnt)
        self._commit(tok, r, w)
        return tok

    def emit(self, nc, handles):
        refd = {e: set() for e in ENGS}
        for e in ENGS:
            for o in self.ops[e]:
                for d in o["deps"]:
                    if d[0] == "e":
                        refd[d[1]].add(d[2])
        cnt = {}
        for e in ENGS:
            c = 0
            for i, o in enumerate(self.ops[e]):
                if i in refd[e]:
                    c += 1
                    cnt[(e, i)] = c
        with ExitStack() as st:
            esem = {e: st.enter_context(nc.semaphore("sem_" + e)) for e in ENGS}
            dsem = {s: st.enter_context(nc.semaphore("dsem_%d" % i)) for i, s in enumerate(self.slotcnt)}
            block = st.enter_context(nc.Block())

            def run(e, h):
                seen = {}
                for i, o in enumerate(self.ops[e]):
                    waits = {}
                    for d in o["deps"]:
                        if d[0] == "e":
                            key, val = ("e", d[1]), cnt[(d[1], d[2])]
                        else:
                            key, val = ("d", d[1]), d[2]
                        if seen.get(key, 0) >= val:
                            continue
                        waits[key] = max(waits.get(key, 0), val)
                    for key, val in waits.items():
                        seen[key] = val
                        h.wait_ge(esem[key[1]] if key[0] == "e" else dsem[key[1]], val)
                    inst = o["fn"](h)
                    if o["dma"] is not None:
                        if o["inc"]:
                            inst.then_inc(dsem[o["dma"]], o["inc"])
                        else:
                            inst.then_inc(dsem[o["dma"]])
                    elif i in refd[e]:
                        inst.then_inc(esem[e], 1)
                if e == "sp":
                    for s, c in self.slotcnt.items():
                        h.wait_ge(dsem[s], c)

            @block.tensor
            def _(h):
                run("pe", h)

            @block.scalar
            def _(h):
                run("act", h)

            @block.vector
            def _(h):
                run("dve", h)

            @block.gpsimd
            def _(h):
                run("pool", h)

            @block.sync
            def _(h):
                run("sp", h)


class Arena:
    def __init__(self, ap, nbytes):
        self.ap = ap
        self.n = nbytes
        self.off = 0

    def alloc(self, shape, dtype, parts=128):
        esz = {F32: 4, BF16: 2, I32: 4}[dtype]
        n = int(np.prod(shape))
        nb = (n * esz + 31) // 32 * 32
        assert self.off + nb <= self.n, ("arena overflow", self.off, nb, self.n)
        v = self.ap[0:parts, self.off:self.off + nb][:, 0:n * esz].bitcast(dtype)
        self.off += nb
        if len(shape) == 2:
            v = v.rearrange("p (a b) -> p a b", b=shape[1])
        elif len(shape) == 3:
            v = v.rearrange("p (a b c) -> p a b c", b=shape[1], c=shape[2])
        return v


def build_program(debug=False, stop=None, nsteps=None):
    nc = bass.Bass("TRN2", target_bir_lowering=False)
    xT = nc.dram_tensor("xT", [DM, SEQ], F32, kind="ExternalInput").ap()
    xB = nc.dram_tensor("xB", [DM, 2050], F32, kind="ExternalInput").ap()
    prm_d = nc.dram_tensor("prm", [128, PW], F32, kind="ExternalInput").ap()
    wada_d = nc.dram_tensor("wada", [12, 128, 8, 512], F32, kind="ExternalInput").ap()
    win_d = nc.dram_tensor("win", [128, 8, 784], F32, kind="ExternalInput").ap()
    wout_d = nc.dram_tensor("wout", [128, 8, 1024], F32, kind="ExternalInput").ap()
    wup_d = nc.dram_tensor("wup", [22, 128, 2, 8, 128], F32, kind="ExternalInput").ap()
    wdn_d = nc.dram_tensor("wdn", [128, 22, 1024], F32, kind="ExternalInput").ap()
    idx_d = nc.dram_tensor("tokoff", [1, 2], I32, kind="ExternalInput").ap()
    yT = nc.dram_tensor("yT", [DM, 2048], F32, kind="ExternalOutput").ap()
    agin = [nc.dram_tensor("agin%d" % k, [256, 2048], BF16) for k in range(4)]
    agout = nc.dram_tensor("agout", [4 * 1024, 2048], BF16)
    if debug:
        dbg_o = nc.dram_tensor("dbg_o", [256, SEQ], F32, kind="ExternalOutput").ap()

    S = Sched()
    st = ExitStack()
    sb = lambda name, shape, dt: st.enter_context(nc.sbuf_tensor(name, shape, dt))
    ps = lambda name, shape, dt=F32: st.enter_context(nc.psum_tensor(name, shape, dt))

    prm = sb("prm_sb", [128, PW], F32)
    cb = sb("cb", [128, 8 * 128], BF16)
    modsb = sb("modsb", [128, 48], F32)
    a1 = sb("a1", [128, 8], F32)
    a2 = sb("a2", [128, 8], F32)
    g1p = sb("g1p", [128, 8], F32)
    g2p = sb("g2p", [128, 8], F32)
    scb = sb("scb", [128, 8], BF16)
    sctmp = sb("sctmp", [128, 16], F32)
    zero2 = sb("zero2", [128, 2, 2], BF16)
    ARENA_BYTES = 198 * 1024
    arena_t = sb("arena", [128, ARENA_BYTES], U8)
    reg_off = st.enter_context(nc.gpsimd.register("tokoff_reg"))
    reg_off2 = st.enter_context(nc.gpsimd.register("tokoff_reg2"))

    bank2 = [ps("bank2_%d" % i, [128, 1024]) for i in range(4)]
    banks = [bank2[i // 2][:, 512 * (i % 2):512 * (i % 2) + 512] for i in range(8)]
    ZZ = bank2[0][:, :].rearrange("p (h c) -> p h c", h=2)
    Z = [banks[0], banks[1]]
    CB = [banks[2], banks[3]]
    OUTB = banks[4]
    PA, PB, PC = banks[5], banks[6], banks[7]

    def cm(k):
        return cb[:, 128 * k:128 * k + 128]

    shift1 = modsb[:, 0:8]
    shift2 = modsb[:, 24:32]

    S.dma("sp", "prm", lambda h: h.dma_start(out=prm[:], in_=prm_d[:, :]), w=["prm"])
    S.op("dve", lambda h: h.tensor_copy(out=cb[:], in_=prm[:, C_CONST:C_CONST + 1024]), r=["prm"], w=["cb"])
    S.op("dve", lambda h: h.memset(zero2[:], 0.0), w=["zero2"])
    S.op("act", lambda h: h.activation(out=sctmp[:, 0:8], in_=prm[:, C_C:C_C + 8], func=AF.Exp, scale=-1.0),
         r=["prm"], w=["sct0"])
    S.op("dve", lambda h: h.tensor_scalar(out=sctmp[:, 8:16], in0=sctmp[:, 0:8], scalar1=1.0, scalar2=None,
                                          op0=ALU.add), r=["sct0"], w=["sct1"])
    S.op("dve", lambda h: h.reciprocal(out=sctmp[:, 0:8], in_=sctmp[:, 8:16]), r=["sct1"], w=["sct0"])
    S.op("dve", lambda h: h.tensor_tensor(out=scb[:], in0=sctmp[:, 0:8], in1=prm[:, C_C:C_C + 8], op=ALU.mult),
         r=["sct0", "prm"], w=["scb"])

    A = Arena(arena_t, ARENA_BYTES)
    qT = A.alloc([SEQ], BF16)
    kT = A.alloc([SEQ], BF16)
    Vtok = A.alloc([NB, 128], BF16)
    dvtok = A.alloc([NB, 128], BF16)
    win = A.alloc([8, 784], BF16)
    wada = [A.alloc([8, 512], BF16) for _ in range(4)]
    NX = 8
    xs = [A.alloc([512], F32) for _ in range(NX)]
    sq = A.alloc([8, 512], BF16)
    hT = A.alloc([8, 512], BF16)
    tmpb = [A.alloc([512], F32) for _ in range(2)]
    lnv = A.alloc([512], F32)
    rstd = A.alloc([512], F32)
    ebm = A.alloc([2, 512], F32)
    eb = [ebm[:, 0, :], ebm[:, 1, :]]
    spm = [A.alloc([2, 512], BF16) for _ in range(2)]
    spb = [[spm[0][:, hd_, :], spm[1][:, hd_, :]] for hd_ in range(2)]
    Eb = [[A.alloc([512], BF16) for _ in range(2)] for _ in range(2)]
    osb = [A.alloc([512], BF16) for _ in range(2)]
    ogl = [A.alloc([512], BF16) for _ in range(2)]
    gqT2 = [A.alloc([512], BF16) for _ in range(2)]
    sg2 = [A.alloc([512], F32) for _ in range(2)]
    gtmp = A.alloc([512], F32)
    gfT2 = [A.alloc([512], F32) for _ in range(2)]
    gvb2 = [[A.alloc([128], BF16) for _ in range(4)] for _ in range(2)]
    gkb2 = [[A.alloc([64], F32) for _ in range(4)] for _ in range(2)]
    ey2 = [A.alloc([64], F32) for _ in range(2)]
    spy2 = [A.alloc([64], F32) for _ in range(2)]
    Dd2 = [A.alloc([64], F32) for _ in range(2)]
    dec2 = [A.alloc([2], F32) for _ in range(2)]
    kdec2 = [A.alloc([64], BF16) for _ in range(2)]
    Sst = [A.alloc([128], F32) for _ in range(2)]
    Sb = [A.alloc([128], BF16) for _ in range(8)]
    sqo = A.alloc([512], BF16)
    lno = A.alloc([512], F32)
    rso = A.alloc([512], F32)
    ot1 = A.alloc([512], F32)
    if debug:
        dbgf = [A.alloc([512], F32) for _ in range(2)]
    print("phase A arena bytes:", A.off)

    def mod_dma(m):
        buf = wada[m % 4]
        key = "wada%d" % (m % 4)
        S.dma("pool", key, lambda h: h.dma_start(out=buf, in_=wada_d[m], max_dma_last_dim=2048), w=[key])

    def mod_compute(m):
        buf = wada[m % 4]
        key = "wada%d" % (m % 4)
        for j in range(4):
            col = 4 * m + j
            for kc in range(8):
                S.op("pe", lambda h, j=j, kc=kc, col=col: h.matmul(
                    PC[:, col:col + 1], lhsT=buf[:, kc, 128 * j:128 * j + 128], rhs=scb[:, kc:kc + 1],
                    start=(kc == 0), stop=(kc == 7)), r=[key, "scb"], w=["PC"])
        S.op("dve", lambda h: h.tensor_tensor(out=modsb[:, 4 * m:4 * m + 4], in0=PC[:, 4 * m:4 * m + 4],
                                              in1=prm[:, C_BADA + 4 * m:C_BADA + 4 * m + 4], op=ALU.add),
             r=["PC", "prm"], w=["mod%d" % m])

    def mod_front():
        for m in range(4):
            mod_compute(m)
        S.op("dve", lambda h: h.scalar_tensor_tensor(out=a1[:], in0=modsb[:, 8:16], scalar=1.0,
                                                     in1=prm[:, C_G1:C_G1 + 8], op0=ALU.add, op1=ALU.mult),
             r=["mod2", "mod3", "prm"], w=["a1"])

    def mod_back_tasks():
        T_ = []
        for m in range(4, 12):
            T_.append((lambda m=m: mod_dma(m), True))
            T_.append((lambda m=m: mod_compute(m), True))

        def fin():
            S.op("dve", lambda h: h.tensor_scalar(out=g1p[:], in0=modsb[:, 16:24], scalar1=1.0, scalar2=None,
                                                  op0=ALU.add), r=["mod4", "mod5"], w=["g1p"])
            S.op("dve", lambda h: h.scalar_tensor_tensor(out=a2[:], in0=modsb[:, 32:40], scalar=1.0,
                                                         in1=prm[:, C_G2:C_G2 + 8], op0=ALU.add, op1=ALU.mult),
                 r=["mod8", "mod9", "prm"], w=["a2"])
            S.op("dve", lambda h: h.tensor_scalar(out=g2p[:], in0=modsb[:, 40:48], scalar1=1.0, scalar2=None,
                                                  op0=ALU.add), r=["mod10", "mod11"], w=["g2p"])
        T_.append((fin, True))
        return T_

    for m in range(4):
        mod_dma(m)

    def load_regs(h):
        h.reg_load(reg_off, idx_d[0:1, 0:1])
        return h.reg_load(reg_off2, idx_d[0:1, 1:2])
    S.op("pool", load_regs)
    S.dma("pool", "win", lambda h: h.dma_start(out=win, in_=win_d[:, :, :], max_dma_last_dim=3136), w=["win"])

    def finish():
        S.emit(nc, None)
        st.close()
        return nc
    if stop == "setup":
        return finish()
    xcount = [0]

    def proj_tasks(j):
        C, G = [], []
        t0 = 512 * j
        qd = j % 2
        gqT, sg, gfT = gqT2[qd], sg2[qd], gfT2[qd]
        kgq, ksg, kgf = "gqT%d" % qd, "sg%d" % qd, "gfT%d" % qd
        xslots = []

        ess = [True]
        gsafe = [True]

        def add(fn, hop=False):
            C.append((fn, hop, ess[0], gsafe[0]))

        def addg(fn, hop=False):
            G.append((fn, hop))

        def t_load():
            for kc in range(8):
                sl = xcount[0] % NX
                xcount[0] += 1
                xslots.append(sl)
                S.dma("sp", "xs%d" % sl, lambda h, kc=kc, sl=sl: h.dma_start(
                    out=xs[sl], in_=xT[128 * kc:128 * kc + 128, t0:t0 + 512]), w=["xs%d" % sl])
        add(t_load)

        def t_sq(kc):
            sl = xslots[kc]
            S.op("pool", lambda h: h.tensor_tensor(out=sq[:, kc, :], in0=xs[sl], in1=xs[sl], op=ALU.mult),
                 r=["xs%d" % sl], w=["sq%d" % kc])
        for kc in range(8):
            add(lambda kc=kc: t_sq(kc), kc == 0)

        def add_split(fn_kcs, first_hop):
            per = 8 // PSPLIT
            for p_ in range(PSPLIT):
                kcs = list(range(per * p_, per * p_ + per))
                add(lambda kcs=kcs: fn_kcs(kcs), first_hop if p_ == 0 else True)

        def t_ssq(kcs):
            for kc in kcs:
                S.op("pe", lambda h, kc=kc: h.matmul(PA[:, :], lhsT=cm(K_ONES), rhs=sq[:, kc, :],
                                                     start=(kc == 0), stop=(kc == 7)),
                     r=["sq%d" % kc, "cb"], w=["PA"])
        add_split(t_ssq, True)

        def t_rstd():
            S.op("act", lambda h: h.activation(out=lnv, in_=PA[:, :], func=AF.Ln, scale=1.0 / DM, bias=EPS),
                 r=["PA"], w=["lnv"])
            S.op("act", lambda h: h.activation(out=rstd, in_=lnv, func=AF.Exp, scale=-0.5), r=["lnv"], w=["rstd"])
        add(t_rstd, True)

        def t_h(kc):
            sl = xslots[kc]
            tb_ = tmpb[kc % 2]
            S.op("dve", lambda h: h.scalar_tensor_tensor(out=tb_, in0=xs[sl], scalar=a1[:, kc:kc + 1], in1=rstd,
                                                         op0=ALU.mult, op1=ALU.mult),
                 r=["xs%d" % sl, "rstd", "a1"], w=["tmp%d" % (kc % 2)])
            S.op("pool", lambda h: h.tensor_scalar(out=hT[:, kc, :], in0=tb_, scalar1=1.0,
                                                   scalar2=shift1[:, kc:kc + 1], op0=ALU.mult, op1=ALU.add),
                 r=["tmp%d" % (kc % 2), "mod0", "mod1"], w=["hT%d" % kc])
        for kc in range(8):
            add(lambda kc=kc: t_h(kc), kc == 0)

        def fm_group(bank, bkey, c0, M, first_hop):
            def part(kcs):
                for kc in kcs:
                    S.op("pe", lambda h, kc=kc: h.matmul(bank[0:M, :], lhsT=win[:, kc, c0:c0 + M], rhs=hT[:, kc, :],
                                                         start=(kc == 0), stop=(kc == 7)),
                         r=["hT%d" % kc, "win"], w=[bkey])
            add_split(part, first_hop)

        fm_group(PB, "PB", 0, 128, True)
        add(lambda: S.op("dve", lambda h: h.tensor_copy(out=qT[:, t0:t0 + 512], in_=PB[:, :]), r=["PB"],
                         w=["qT%d" % j]), True)
        fm_group(PA, "PA", 128, 128, False)
        add(lambda: S.op("dve", lambda h: h.tensor_copy(out=kT[:, t0:t0 + 512], in_=PA[:, :]), r=["PA"],
                         w=["kT%d" % j]), True)

        def t_tok(tb, part):
            blk = 4 * j + tb
            bank, bkey = (PA, "PA") if tb % 2 == 0 else (PB, "PB")
            gv_ = gvb2[qd][tb]
            gk_ = gkb2[qd][tb]
            kgv, kgk = "gv%d_%d" % (qd, tb), "gk%d_%d" % (qd, tb)
            dvreg = bank[:, 320:448]
            if isinstance(part, tuple):
                for kc in part[1]:
                    S.op("pe", lambda h, kc=kc: h.matmul(bank[:, 0:320], lhsT=hT[:, kc, 128 * tb:128 * tb + 128],
                                                         rhs=win[:, kc, 464:784], start=(kc == 0), stop=(kc == 7)),
                         r=["hT%d" % kc, "win"], w=[bkey])
            elif part == 1:
                S.op("dve", lambda h: h.tensor_copy(out=Vtok[:, blk, :], in_=bank[:, 0:128]), r=[bkey],
                     w=["V%d" % blk])
                S.op("dve", lambda h: h.tensor_copy(out=gv_, in_=bank[:, 192:320]), r=[bkey], w=[kgv])
                S.op("dve", lambda h: h.tensor_copy(out=gk_, in_=bank[:, 128:192]), r=[bkey], w=[kgk])
            elif part == 11:
                S.op("pe", lambda h: h.matmul(dvreg, lhsT=cm(K_DMA), rhs=Vtok[:, blk, :], start=True,
                                              stop=(blk == 0)), r=["V%d" % blk, "cb"], w=[bkey])
                if blk > 0:
                    S.op("pe", lambda h: h.matmul(dvreg, lhsT=cm(K_DMB), rhs=Vtok[:, blk - 1, :], start=False,
                                                  stop=True), r=["V%d" % (blk - 1), "cb"], w=[bkey])
            elif part == 12:
                S.op("dve", lambda h: h.tensor_copy(out=dvtok[:, blk, :], in_=dvreg), r=[bkey], w=["dv%d" % blk])

        for t2_ in (0, 2):
            add_split(lambda kcs, t=t2_: t_tok(t, (0, kcs)), True)
            add_split(lambda kcs, t=t2_: t_tok(t + 1, (0, kcs)), False)
            for part in (1, 11, 12):
                gsafe[0] = False
                add(lambda part=part, t=t2_: t_tok(t, part), True)
                add(lambda part=part, t=t2_: t_tok(t + 1, part))

        ess[0] = False
        fm_group(PB, "PB", 256, 64, True)
        add(lambda: S.op("dve", lambda h: h.tensor_scalar(out=gqT[0:64, :], in0=PB[0:64, :], scalar1=0.125,
                                                          scalar2=None, op0=ALU.mult), r=["PB"], w=[kgq]), True)
        fm_group(PA, "PA", 320, 128, False)
        add(lambda: S.op("act", lambda h: h.activation(out=gtmp, in_=PA[:, :], func=AF.Exp, scale=-1.0),
                         r=["PA"], w=["gtmp"]), True)

        def t_gg_dve():
            S.op("dve", lambda h: h.tensor_scalar(out=sg, in0=gtmp, scalar1=1.0, scalar2=None, op0=ALU.add),
                 r=["gtmp"], w=[ksg])
            S.op("dve", lambda h: h.reciprocal(out=gtmp, in_=sg), r=[ksg], w=["gtmp"])
            S.op("dve", lambda h: h.tensor_tensor(out=sg, in0=gtmp, in1=PA[:, :], op=ALU.mult),
                 r=["gtmp", "PA"], w=[ksg])
        add(t_gg_dve, True)
        fm_group(PB, "PB", 448, 16, False)
        add(lambda: S.op("dve", lambda h: h.tensor_copy(out=gfT[0:16, :], in_=PB[0:16, :]), r=["PB"], w=[kgf]),
            True)

        def t_gla(tb, part):
            pp = tb % 2
            gv_ = gvb2[qd][tb]
            gk_ = gkb2[qd][tb]
            kgv, kgk = "gv%d_%d" % (qd, tb), "gk%d_%d" % (qd, tb)
            ey, spy, Dd, dec, kdec = ey2[pp], spy2[pp], Dd2[pp], dec2[pp], kdec2[pp]
            kE, kS, kD, kC, kK = "ey%d" % pp, "spy%d" % pp, "Dd%d" % pp, "dec%d" % pp, "kdec%d" % pp
            yb = 256 + 128 * pp
            yreg = PC[:, yb:yb + 64]
            dreg = PC[:, yb + 64:yb + 128]
            treg = PC[0:64, 128 * pp:128 * pp + 2]
            if part == 2:
                S.op("pe", lambda h: h.matmul(yreg, lhsT=gfT[0:16, 128 * tb:128 * tb + 128],
                                              rhs=prm[0:16, C_WFG:C_WFG + 64], start=True, stop=False),
                     r=[kgf, "prm"], w=["PC"])
                S.op("pe", lambda h: h.matmul(yreg, lhsT=prm[0:1, C_ONESF:C_ONESF + 128],
                                              rhs=prm[0:1, C_BFG:C_BFG + 64], start=False, stop=True),
                     r=["prm"], w=["PC"])
            elif part == 3:
                S.op("act", lambda h: h.activation(out=ey, in_=yreg, func=AF.Exp, scale=-1.0), r=["PC"], w=[kE])
                S.op("act", lambda h: h.activation(out=spy, in_=ey, func=AF.Ln, bias=1.0), r=[kE], w=[kS])
            elif part == 4:
                S.op("pe", lambda h: h.matmul(dreg, lhsT=prm[:, C_MSUF:C_MSUF + 128], rhs=spy,
                                              start=True, stop=True), r=[kS, "prm"], w=["PC"])
                S.op("pe", lambda h: h.matmul(treg, lhsT=spy, rhs=prm[:, C_CSEL:C_CSEL + 2],
                                              start=True, stop=True), r=[kS, "prm"], w=["PC"])
            elif part == 5:
                S.op("act", lambda h: h.activation(out=Dd, in_=dreg, func=AF.Exp), r=["PC"], w=[kD])
                S.op("act", lambda h: h.activation(out=dec[0:64, :], in_=treg, func=AF.Exp), r=["PC"], w=[kC])
            elif part == 6:
                S.op("dve", lambda h: h.tensor_tensor(out=kdec, in0=gk_, in1=Dd, op=ALU.mult),
                     r=[kgk, kD], w=[kK])
            elif part in (7, 9):
                c = (part - 7) // 2
                S.op("pe", lambda h: h.matmul(PC[0:64, 128 * c:128 * c + 128], lhsT=kdec[64 * c:64 * c + 64, :],
                                              rhs=gv_[64 * c:64 * c + 64, :], start=True, stop=True),
                     r=[kK, kgv], w=["PC"])
            elif part in (8, 10):
                c = (part - 8) // 2
                ci = 2 * tb + c
                so, sn = Sst[ci % 2], Sst[(ci + 1) % 2]
                if j == 0 and ci == 0:
                    S.op("dve", lambda h: h.tensor_copy(out=sn[0:64, :], in_=PC[0:64, 128 * c:128 * c + 128]),
                         r=["PC"], w=["S%d" % ((ci + 1) % 2)])
                else:
                    S.op("dve", lambda h: h.scalar_tensor_tensor(
                        out=sn[0:64, :], in0=so[0:64, :], scalar=dec[0:64, c:c + 1], in1=PC[0:64, 128 * c:128 * c + 128],
                        op0=ALU.mult, op1=ALU.add),
                        r=["PC", kC, "S%d" % (ci % 2)], w=["S%d" % ((ci + 1) % 2)])
                S.op("dve", lambda h: h.tensor_copy(out=Sb[ci][0:64, :], in_=sn[0:64, :]),
                     r=["S%d" % ((ci + 1) % 2)], w=["Sb%d" % ci])

        for t2_ in (0, 2):
            for part in range(2, 7):
                addg(lambda part=part, t=t2_: t_gla(t, part), True)
                addg(lambda part=part, t=t2_: t_gla(t + 1, part))
            for tb in (t2_, t2_ + 1):
                for part in (7, 8, 9, 10):
                    addg(lambda tb=tb, part=part: t_gla(tb, part), True)

        def glao_pe(hf):
            for ci in range(4 * hf, 4 * hf + 4):
                cl = ci - 4 * hf
                S.op("pe", lambda h, ci=ci, cl=cl: h.matmul(PC[:, 64 * cl:64 * cl + 64], lhsT=Sb[ci][0:64, :],
                                                            rhs=gqT[0:64, 64 * ci:64 * ci + 64], start=True, stop=True),
                     r=["Sb%d" % ci, kgq], w=["PC"])

        def gla_rs(hf):
            S.op("act", lambda h: h.activation(out=lno[:, 0:256], in_=PC[:, 256:512], func=AF.Ln, scale=1.0 / 128,
                                               bias=EPS), r=["PC"], w=["lno"])
            S.op("act", lambda h: h.activation(out=rso[:, 0:256], in_=lno[:, 0:256], func=AF.Exp, scale=-0.5),
                 r=["lno"], w=["rso"])

        def gla_mid(hf):
            S.op("dve", lambda h: h.tensor_tensor(out=ot1[:, 256 * hf:256 * hf + 256], in0=PC[:, 0:256],
                                                  in1=rso[:, 0:256], op=ALU.mult), r=["PC", "rso"], w=["ot1"])
        for hf in range(2):
            addg(lambda hf=hf: glao_pe(hf), True)
            addg(lambda hf=hf: S.op("act", lambda h: h.activation(out=sqo[:, 0:256], in_=PC[:, 0:256],
                                                                  func=AF.Square), r=["PC"], w=["sqo"]), True)
            addg(lambda hf=hf: S.op("pe", lambda h: h.matmul(PC[:, 256:512], lhsT=cm(K_ONES), rhs=sqo[:, 0:256],
                                                             start=True, stop=True), r=["sqo", "cb"], w=["PC"]), True)
            addg(lambda hf=hf: gla_rs(hf), True)
            addg(lambda hf=hf: gla_mid(hf), True)

        def gla_fin():
            ob = ogl[j % 2]
            S.op("dve", lambda h: h.scalar_tensor_tensor(out=ob, in0=ot1, scalar=prm[:, C_GGLA:C_GGLA + 1], in1=sg,
                                                         op0=ALU.mult, op1=ALU.mult),
                 r=["ot1", ksg, "prm"], w=["ogl%d" % (j % 2)])
            S.dma("sp", "ogl%d" % (j % 2), lambda h: h.dma_start(
                out=agin[j // 4][128:256, 512 * (j % 4):512 * (j % 4) + 512], in_=ob),
                  r=["ogl%d" % (j % 2)], w=["agin_g%d" % j])
            if debug:
                S.op("dve", lambda h: h.tensor_copy(out=dbgf[1], in_=ob), r=["ogl%d" % (j % 2)], w=["dbgf1"])
                S.dma("sp", "dbgf1", lambda h: h.dma_start(out=dbg_o[128:256, t0:t0 + 512], in_=dbgf[1]),
                      r=["dbgf1"], w=["dbgo_g%d" % j])
        addg(gla_fin, True)
        return C, G

    steps = [(i, n) for i in range(NT) for n in range(4 * i + 3, -1, -1)]
    NS = len(steps)

    def c0_of(i, n):
        return 128 * max(n - 4 * i, 0)

    def stage_qk(k):
        i, n = steps[k]
        c0 = c0_of(i, n)
        for hd in range(2):
            S.op("pe", lambda h, hd=hd: h.matmul(
                Z[hd][:, c0:512], lhsT=kT[64 * hd:64 * hd + 64, 128 * n:128 * n + 128],
                rhs=qT[64 * hd:64 * hd + 64, 512 * i + c0:512 * i + 512], start=True, stop=True),
                r=["kT%d" % (n // 4), "qT%d" % i], w=["Z%d" % hd])

    def stage1(k):
        i, n = steps[k]
        c0 = c0_of(i, n)
        b = k % 2
        diag = n >= 4 * i
        S.op("act", lambda h: h.activation(out=ebm[:, :, c0:512], in_=ZZ[:, :, c0:512], func=AF.Exp, scale=0.125),
             r=["Z0", "Z1"], w=["e0", "e1"])
        S.op("act", lambda h: h.activation(out=spm[b][:, :, c0:512], in_=ebm[:, :, c0:512], func=AF.Ln, bias=1.0),
             r=["e0", "e1"], w=["sp0%d" % b, "sp1%d" % b])
        if diag:
            for hd in range(2):
                S.op("pool", lambda h, hd=hd: h.tensor_tensor(out=spb[hd][b][:, c0:c0 + 128],
                                                               in0=spb[hd][b][:, c0:c0 + 128], in1=cm(K_MASK),
                                                               op=ALU.mult),
                     r=["sp%d%d" % (hd, b), "cb"], w=["sp%d%d" % (hd, b)])

    def stage2(k, first_pe=True):
        i, n = steps[k]
        c0 = c0_of(i, n)
        b = k % 2
        diag = n >= 4 * i
        first = (n == 4 * i + 3)
        for hd in range(2):
            S.op("act", lambda h, hd=hd: h.activation(out=Eb[hd][b][:, c0:512], in_=CB[hd][:, c0:512], func=AF.Exp),
                 r=["CB%d" % hd], w=["E%d%d" % (hd, b)])
        if diag:
            for hd in range(2):
                S.op("pool", lambda h, hd=hd: h.tensor_tensor(out=Eb[hd][b][:, c0:c0 + 128],
                                                               in0=Eb[hd][b][:, c0:c0 + 128], in1=cm(K_MASK),
                                                               op=ALU.mult),
                     r=["E%d%d" % (hd, b), "cb"], w=["E%d%d" % (hd, b)])
        for hd in range(2):
            hs = slice(64 * hd, 64 * hd + 64)
            if first:
                for a in range(4):
                    blk = 4 * i + a
                    S.op("pe", lambda h, a=a, blk=blk, hs=hs: h.matmul(
                        OUTB[hs, 128 * a:128 * a + 128], lhsT=Vtok[:, blk, hs], rhs=cm(K_SHA),
                        start=(a == 0), stop=True, skip_group_check=True),
                        r=["V%d" % blk, "cb"], w=["OUT%d" % hd])
                    if blk > 0:
                        S.op("pe", lambda h, a=a, blk=blk, hs=hs: h.matmul(
                            OUTB[hs, 128 * a:128 * a + 1], lhsT=Vtok[:, blk - 1, hs], rhs=cb[:, 128 * K_SHB:128 * K_SHB + 1],
                            start=False, stop=True, skip_group_check=True),
                            r=["V%d" % (blk - 1), "cb"], w=["OUT%d" % hd])
            if n > 0:
                S.op("pe", lambda h, hd=hd: h.matmul(CB[hd][:, c0:512], lhsT=cm(K_TRIB), rhs=spb[hd][b][:, c0:512],
                                                     start=False, stop=True, skip_group_check=True),
                     r=["sp%d%d" % (hd, b), "E%d%d" % (hd, b), "cb"], w=["CB%d" % hd])
            S.op("pe", lambda h, hd=hd, hs=hs: h.matmul(OUTB[hs, c0:512], lhsT=dvtok[:, n, hs], rhs=Eb[hd][b][:, c0:512],
                                                        start=False, stop=True, skip_group_check=True),
                 r=["E%d%d" % (hd, b), "dv%d" % n], w=["OUT%d" % hd])
        if n == 0:
            ob = osb[i % 2]
            S.op("dve", lambda h: h.tensor_copy(out=ob, in_=OUTB[:, :]), r=["OUT0", "OUT1"], w=["osb%d" % (i % 2)])
            S.dma("sp", "osb%d" % (i % 2), lambda h: h.dma_start(
                out=agin[i // 4][0:128, 512 * (i % 4):512 * (i % 4) + 512], in_=ob),
                  r=["osb%d" % (i % 2)], w=["agin_s%d" % i])
            if i % 4 == 3:
                issue_cc(i // 4)
            if False:
                kq = i // 4
                S.dma("pool", "cc%d" % kq, lambda h: h.collective_compute(
                    "AllGather", ALU.bypass, replica_groups=[[0, 1, 2, 3], [4, 5, 6, 7]],
                    ins=[agin[kq].ap().opt()], outs=[agout[1024 * kq:1024 * kq + 1024, :].opt()]),
                    r=["agin_s%d" % t_ for t_ in range(4 * kq, 4 * kq + 4)] + ["agin_g%d" % t_ for t_ in range(4 * kq, 4 * kq + 4)],
                    w=["agout%d" % kq], inc=None)
            if debug:
                S.op("dve", lambda h: h.tensor_copy(out=dbgf[0], in_=OUTB[:, :]), r=["OUT0", "OUT1"], w=["dbgf0"])
                S.dma("sp", "dbgf0", lambda h: h.dma_start(out=dbg_o[0:128, 512 * i:512 * i + 512], in_=dbgf[0]),
                      r=["dbgf0"], w=["dbgo_s%d" % i])

    def issue_cc(kq):
        tl_ = 4 * kq + 3
        while pq and pq[0][0] <= tl_:
            pq.popleft()[1]()
        while gqueue and gqueue[0][0] <= tl_:
            gqueue.popleft()[1]()
        if os.environ.get("NOCC"):
            return
        S.dma("pool", "cc%d" % kq, lambda h: h.collective_compute(
            "AllGather", ALU.bypass, replica_groups=[[0, 1, 2, 3], [4, 5, 6, 7]],
            ins=[agin[kq].ap().opt()], outs=[agout[1024 * kq:1024 * kq + 1024, :].opt()]),
            r=["agin_s%d" % t_ for t_ in range(4 * kq, 4 * kq + 4)] + ["agin_g%d" % t_ for t_ in range(4 * kq, 4 * kq + 4)],
            w=["agout%d" % kq], inc=None)

    def stage_c1(k):
        i, n = steps[k]
        c0 = c0_of(i, n)
        b = k % 2
        first = (n == 4 * i + 3)
        for hd in range(2):
            S.op("pe", lambda h, hd=hd: h.matmul(CB[hd][:, c0:512], lhsT=cm(K_TRIA), rhs=spb[hd][b][:, c0:512],
                                                 start=first, stop=True, skip_group_check=True),
                 r=["sp%d%d" % (hd, b), "cb"], w=["CB%d" % hd])

    bonly = (stop == "Bonly")
    if bonly:
        nsteps = 0
    gqueue = deque()
    if bonly:
        mod_front()
        for fn_, _h in mod_back_tasks():
            fn_()
    else:
        c0_, g0_ = proj_tasks(0)
        npre = 1 + 8 + PSPLIT + 1
        for t, _hop, _e, _s in c0_[:npre]:
            t()
        mod_front()
        for t, _hop, _e, _s in c0_[npre:]:
            t()
        for fn_, hop_ in g0_:
            gqueue.append((0, fn_, hop_))
    if stop == "proj0":
        while gqueue:
            gqueue.popleft()[1]()
        return finish()
    pq = deque()
    if nsteps is not None:
        steps = steps[:nsteps]
        NS = len(steps)
    if not bonly:
        stage_qk(0)
    if not bonly:
        for t_ in range(1, NT):
            c_, g_ = proj_tasks(t_)
            for fn_, hop_, e_, s_ in c_:
                pq.append((t_, fn_, hop_, e_, s_))
            for fn_, hop_ in g_:
                gqueue.append((t_, fn_, hop_))
            if t_ == 7:
                for fn_, hop_ in mod_back_tasks():
                    gqueue.append((t_, fn_, hop_))

    def catch_up(task):
        lim = task[0] - 2
        if len(task) > 4 and task[4]:
            if gqueue and gqueue[0][0] <= lim:
                gqueue.popleft()[1]()
                while gqueue and not gqueue[0][2] and gqueue[0][0] <= lim:
                    gqueue.popleft()[1]()
        else:
            while gqueue and gqueue[0][0] <= lim:
                gqueue.popleft()[1]()

    def drain_group(qu):
        if not qu:
            return
        if qu is pq:
            catch_up(qu[0])
        qu.popleft()[1]()
        while qu and not qu[0][2]:
            qu.popleft()[1]()

    def drain_paced(cur_tile, slots_left, background):
        if not pq:
            return
        need = 0
        for idx_, _t in enumerate(pq):
            if _t[0] > cur_tile + 1 or (_t[0] == cur_tile + 1 and not _t[3]):
                break
            if _t[2] or idx_ == 0:
                need += 1
        if need > 0:
            g = -(-need // max(slots_left, 1))
            for _ in range(g):
                drain_group(pq)
        elif background:
            drain_group(pq)

    def drain_gla(cur_tile, steps_left_total, steps_left_tile=None):
        if not gqueue:
            return
        if LAZY and cur_tile >= LAZY_FROM:
            ngr = sum(1 for idx_, _t in enumerate(gqueue) if _t[0] <= cur_tile and (_t[2] or idx_ == 0))
            g = -(-ngr // max(steps_left_tile, 1))
        else:
            ngr = sum(1 for idx_, _t in enumerate(gqueue) if _t[2] or idx_ == 0)
            g = -(-ngr // max(steps_left_total, 1))
        for _ in range(g):
            if gqueue and (not pq or pq[0][0] > gqueue[0][0]):
                drain_group(gqueue)

    def force_drain(tile):
        while pq and (pq[0][0] < tile or (pq[0][0] == tile and pq[0][3])):
            catch_up(pq[0])
            pq.popleft()[1]()

    for k in range(NS + 1 if not bonly else 0):
        if k < NS:
            i, n = steps[k]
            if LAZY and i >= LAZY_FROM:
                drain_paced(i, n + 1, False)
            else:
                drain_paced(i, 2 * n + 2, True)
            stage1(k)
        if k + 1 < NS:
            ni, nn = steps[k + 1]
            if nn == 4 * ni + 3:
                force_drain(ni)
            stage_qk(k + 1)
        if k < NS:
            drain_gla(i, NS - k, n + 1)
        if k >= 1:
            stage2(k - 1)
        if k < NS:
            stage_c1(k)
    while pq:
        pq.popleft()[1]()
    while gqueue:
        gqueue.popleft()[1]()

    if nsteps is not None and not bonly:
        for kq in range(4):
            issue_cc(kq)
    if stop == "A":
        return finish()
    S.barrier("pool", w=["phaseA_done"])

    if stop == "AG":
        return finish()
    B = Arena(arena_t, ARENA_BYTES)
    woutb = B.alloc([8, 1024], BF16)
    wdnb = B.alloc([22, 1024], BF16)
    NWU = 6
    wupb = [B.alloc([2, 8, 128], BF16) for _ in range(NWU)]
    oall = B.alloc([8, 2050], BF16)
    xb = B.alloc([8, TB + 2], F32)
    x1 = B.alloc([8, TB + 2], F32)
    sq2 = B.alloc([8, TB + 2], BF16)
    h2 = B.alloc([8, TB + 2], BF16)
    lnb = B.alloc([TB + 2], F32)
    rsb = B.alloc([TB + 2], F32)
    tmp2 = [B.alloc([TB + 2], F32) for _ in range(2)]
    accv = [B.alloc([TB], F32) for _ in range(2)]
    accg = [B.alloc([TB], F32) for _ in range(2)]
    sgb = [B.alloc([TB], F32) for _ in range(2)]
    aT = B.alloc([22, TB], BF16)
    x2 = xb
    yo = x1
    print("phase B arena bytes:", B.off)
    U = [banks[0], banks[1], banks[2], banks[3]]
    MB = [banks[4], banks[5]]
    SSB = banks[6]
    YB = [banks[7], banks[6]]

    pa = ["phaseA_done"]
    S.dma("pool", "wout", lambda h: h.dma_start(out=woutb, in_=wout_d[:, :, :], max_dma_last_dim=4096), r=pa, w=["wout"])

    def load_wdn():
        for jj in range(2):
            S.dma("pool", "wdn%d" % jj, lambda h, jj=jj: h.dma_start(out=wdnb[:, 11 * jj:11 * jj + 11, :],
                                                                     in_=wdn_d[:, 11 * jj:11 * jj + 11, :], max_dma_last_dim=4096),
                  r=pa, w=["wdn%d" % jj])

    agv = agout.ap().rearrange("(c p) w -> p c w", p=128)

    def load_off(h):
        off = h.snap(reg_off)
        return h.dma_start(out=oall[:, :, 2:2050], in_=agv[:, bass.ds(off, 8), :])

    def load_off2(h):
        off2 = h.snap(reg_off2)
        return h.dma_start(out=oall[:, :, 0:2], in_=agv[:, bass.ds(off2, 8), 2046:2048])
    agk = ["agout%d" % kq for kq in range(4)]
    if bonly:
        S.op("pool", lambda h: h.memset(oall, 0.0), r=pa, w=["oall", "oallh"])
    else:
        S.dma("pool", "oall", load_off, r=agk + pa, w=["oall"])
        S.dma("pool", "oallh", load_off2, r=agk + pa, w=["oallh"])

    wucount = [0]
    NTB = 5

    wu_issued = [0]

    def issue_wup():
        gi = wu_issued[0]
        if gi >= 22 * NTB:
            return
        wu_issued[0] += 1
        sl_ = gi % NWU
        jp_ = gi % 22
        S.dma("pool", "wup%d" % sl_, lambda h: h.dma_start(out=wupb[sl_], in_=wup_d[jp_], max_dma_last_dim=4096),
              r=pa, w=["wup%d" % sl_])
    for _ in range(NWU):
        issue_wup()
    load_wdn()

    def tile_body(tl):
        cin0 = TB * tl
        nin = min(TB + 2, 2050 - cin0)
        nout = nin - 2
        xbk = ["xb_%d" % fc for fc in range(8)]
        x1k = ["x1_%d" % fc for fc in range(8)]
        S.dma("sp", "xb", lambda h: h.dma_start(
            out=xb[:, :, 0:nin], in_=xB.rearrange("(c p) w -> p c w", p=128)[:, :, cin0:cin0 + nin]),
            r=pa, w=xbk)

        def mix_fc(fc):
            mb, mk = MB[fc % 2], "MB%d" % (fc % 2)

            def mm(c):
                wrow = (c // 2) if c % 2 == 0 else 4 + (c // 2)
                S.op("pe", lambda h: h.matmul(
                    mb[:, 0:nin], lhsT=woutb[:, wrow, 128 * fc:128 * fc + 128], rhs=oall[:, c, cin0:cin0 + nin],
                    start=(c == 0), stop=(c == 7)), r=["wout", "oall", "oallh"], w=[mk])
            for c in range(8):
                mm(c)
            S.op("dve", lambda h: h.scalar_tensor_tensor(
                out=x1[:, fc, 0:nin], in0=mb[:, 0:nin], scalar=g1p[:, fc:fc + 1], in1=xb[:, fc, 0:nin],
                op0=ALU.mult, op1=ALU.add), r=[mk, "xb_%d" % fc, "g1p"], w=["x1_%d" % fc])
            S.op("pool", lambda h: h.tensor_tensor(out=sq2[:, fc, 0:nin], in0=x1[:, fc, 0:nin],
                                                   in1=x1[:, fc, 0:nin], op=ALU.mult),
                 r=["x1_%d" % fc], w=["sq2_%d" % fc])
        for fc in range(8):
            mix_fc(fc)

        def ssq(n_, fc):
            S.op("pe", lambda h: h.matmul(SSB[:, 0:n_], lhsT=cm(K_ONES), rhs=sq2[:, fc, 0:n_],
                                          start=(fc == 0), stop=(fc == 7)), r=["sq2_%d" % fc, "cb"], w=["SSB"])

        def rstd_of(n_):
            for fc in range(8):
                ssq(n_, fc)
            S.op("act", lambda h: h.activation(out=lnb[:, 0:n_], in_=SSB[:, 0:n_], func=AF.Ln, scale=1.0 / DM, bias=EPS),
                 r=["SSB"], w=["lnb"])
            S.op("act", lambda h: h.activation(out=rsb[:, 0:n_], in_=lnb[:, 0:n_], func=AF.Exp, scale=-0.5),
                 r=["lnb"], w=["rsb"])
        rstd_of(nin)

        def h2_fc(fc):
            t2 = tmp2[fc % 2]
            S.op("dve", lambda h: h.scalar_tensor_tensor(
                out=t2[:, 0:nin], in0=x1[:, fc, 0:nin], scalar=a2[:, fc:fc + 1], in1=rsb[:, 0:nin],
                op0=ALU.mult, op1=ALU.mult), r=["x1_%d" % fc, "rsb", "a2"], w=["tmp2_%d" % (fc % 2)])
            S.op("act", lambda h: h.activation(
                out=h2[:, fc, 0:nin], in_=t2[:, 0:nin], func=AF.Identity, bias=shift2[:, fc:fc + 1], scale=1.0),
                r=["tmp2_%d" % (fc % 2), "mod6", "mod7"], w=["h2_%d" % fc])
        for fc in range(8):
            h2_fc(fc)
        h2k = ["h2_%d" % fc for fc in range(8)]
        if tl == 0:
            S.op("dve", lambda h: h.tensor_scalar(out=h2[:, :, 0:2], in0=h2[:, :, 0:2],
                                                  scalar1=prm[:, C_HALO:C_HALO + 1], scalar2=None, op0=ALU.mult),
                 r=h2k + ["prm"], w=h2k)

        def pair(jp):
            sl = wucount[0] % NWU
            wucount[0] += 1
            wk = "wup%d" % sl
            ub = [U[(2 * jp) % 4], U[(2 * jp + 1) % 4]]
            uk = ["U%d" % ((2 * jp) % 4), "U%d" % ((2 * jp + 1) % 4)]
            accs = [accv[jp % 2], accg[jp % 2]]
            acck = ["accv%d" % (jp % 2), "accg%d" % (jp % 2)]

            def upmm(vg, fc):
                S.op("pe", lambda h: h.matmul(
                    ub[vg][:, 0:nin], lhsT=wupb[sl][:, vg, fc, :], rhs=h2[:, fc, 0:nin],
                    start=(fc == 0), stop=(fc == 7)), r=[wk, "h2_%d" % fc], w=[uk[vg]])
            for vg in range(2):
                for fc in range(8):
                    upmm(vg, fc)
            issue_wup()

            def conv(vg):
                ch = jp + 22 * vg
                wc = lambda t_: prm[:, C_WC + 44 * t_ + ch:C_WC + 44 * t_ + ch + 1]
                S.op("act", lambda h: h.activation(
                    out=accs[vg][:, 0:nout], in_=ub[vg][:, 2:nin], func=AF.Identity, scale=wc(2),
                    bias=prm[:, C_BC + ch:C_BC + ch + 1]), r=[uk[vg], "prm"], w=[acck[vg]])
                S.op("dve", lambda h: h.scalar_tensor_tensor(
                    out=accs[vg][:, 0:nout], in0=ub[vg][:, 1:nin - 1], scalar=wc(1), in1=accs[vg][:, 0:nout],
                    op0=ALU.mult, op1=ALU.add), r=[uk[vg], acck[vg], "prm"], w=[acck[vg]])
                S.op("dve", lambda h: h.scalar_tensor_tensor(
                    out=accs[vg][:, 0:nout], in0=ub[vg][:, 0:nin - 2], scalar=wc(0), in1=accs[vg][:, 0:nout],
                    op0=ALU.mult, op1=ALU.add), r=[uk[vg], acck[vg], "prm"], w=[acck[vg]])
            for vg in range(2):
                conv(vg)
            sgt = sgb[jp % 2]
            S.op("act", lambda h: h.activation(out=sgt[:, 0:nout], in_=accs[1][:, 0:nout], func=AF.Silu),
                 r=[acck[1]], w=["sgb%d" % (jp % 2)])
            S.op("pool", lambda h: h.tensor_tensor(out=aT[:, jp, 0:nout], in0=accs[0][:, 0:nout],
                                                   in1=sgt[:, 0:nout], op=ALU.mult),
                 r=[acck[0], "sgb%d" % (jp % 2)], w=["aT%d" % jp])
        for jp in range(22):
            pair(jp)

        def down_fc(fc):
            ybk, yk = (YB[0], "YB0") if fc % 2 == 0 else (MB[0], "MB0")

            def mm(jp):
                S.op("pe", lambda h: h.matmul(
                    ybk[:, 0:nout], lhsT=wdnb[:, jp, 128 * fc:128 * fc + 128], rhs=aT[:, jp, 0:nout],
                    start=(jp == 0), stop=(jp == 21)), r=["aT%d" % jp, "wdn%d" % (jp // 11)], w=[yk])
            for jp in range(22):
                mm(jp)
            S.op("dve", lambda h: h.scalar_tensor_tensor(
                out=x2[:, fc, 0:nout], in0=ybk[:, 0:nout], scalar=g2p[:, fc:fc + 1], in1=x1[:, fc, 2:nin],
                op0=ALU.mult, op1=ALU.add), r=[yk, "x1_%d" % fc, "g2p"], w=["xb_%d" % fc])
            S.op("pool", lambda h: h.tensor_tensor(out=sq2[:, fc, 0:nout], in0=x2[:, fc, 0:nout],
                                                   in1=x2[:, fc, 0:nout], op=ALU.mult),
                 r=["xb_%d" % fc], w=["sq2_%d" % fc])
        for fc in range(8):
            down_fc(fc)
        rstd_of(nout)

        def fin_fc(fc):
            S.op("dve", lambda h: h.scalar_tensor_tensor(
                out=yo[:, fc, 0:nout], in0=x2[:, fc, 0:nout], scalar=prm[:, C_GF + fc:C_GF + fc + 1], in1=rsb[:, 0:nout],
                op0=ALU.mult, op1=ALU.mult), r=["xb_%d" % fc, "rsb", "prm"], w=["x1_%d" % fc])
        for fc in range(8):
            fin_fc(fc)
        S.dma("sp", "yo", lambda h: h.dma_start(
            out=yT.rearrange("(c p) w -> p c w", p=128)[:, :, cin0:cin0 + nout], in_=yo[:, :, 0:nout]),
            r=x1k, w=["yT%d" % tl])

    for tl in range(NTB):
        tile_body(tl)

    return finish()


def _consts():
    j = np.arange(128)[:, None]
    s = np.arange(128)[None, :]
    m = np.zeros((8, 128, 128), np.float32)
    m[K_TRIA] = -1.0 * (j >= s)
    m[K_TRIB] = -1.0 * (j < s)
    m[K_MASK] = (j < s)
    m[K_SHA] = (j == s - 1)
    m[K_SHB][127, 0] = 1.0
    m[K_DMA] = (j == s - 1).astype(np.float32) - (j == s).astype(np.float32)
    m[K_DMB][127, 0] = 1.0
    m[K_ONES] = 1.0
    msuf = (-(1.0 / 16.0) * ((j > s) & ((j // 64) == (s // 64)))).astype(np.float32)
    csel = np.zeros((128, 2), np.float32)
    csel[:64, 0] = -1.0 / 16.0
    csel[64:, 1] = -1.0 / 16.0
    return m, msuf, csel


def _pcol(v):
    return np.ascontiguousarray(v.reshape(-1, 128).T)


_NC_CACHE = {}


def kernel(x, c, w_ada, b_ada, g_norm1, w_in, w_fg2, b_fg2, g_gla_out, w_out, g_norm2, w_up, w_conv, b_conv,
           w_down, g_final, _debug=False, _stop=None, _nsteps=None):
    f = lambda a: np.asarray(a, dtype=np.float32)
    x, c, w_ada, b_ada, g_norm1, w_in = f(x), f(c), f(w_ada)[0], f(b_ada)[0], f(g_norm1)[0], f(w_in)[0]
    w_fg2, b_fg2, g_gla_out, w_out = f(w_fg2)[0], f(b_fg2)[0], f(g_gla_out)[0], f(w_out)[0]
    g_norm2, w_up, w_conv, b_conv, w_down, g_final = f(g_norm2)[0], f(w_up)[0], f(w_conv)[0], f(b_conv)[0], f(w_down)[0], f(g_final)
    m8, msuf, csel = _consts()
    key = (bool(_debug), _stop, _nsteps)
    if key not in _NC_CACHE:
        _NC_CACHE[key] = build_program(debug=_debug, stop=_stop, nsteps=_nsteps)
    nc = _NC_CACHE[key]

    wada_l = np.ascontiguousarray(w_ada.reshape(8, 128, 12, 512).transpose(2, 1, 0, 3))
    wout_l = np.ascontiguousarray(w_out.reshape(8, 128, 1024).transpose(1, 0, 2))
    wdn_l = np.ascontiguousarray(w_down.reshape(22, 128, 1024).transpose(1, 0, 2))
    wu = w_up.reshape(8, 128, 2, 22, 128)
    wup_l = np.ascontiguousarray(wu.transpose(3, 1, 2, 0, 4))
    xTs = [np.ascontiguousarray(x[b].T) for b in range(2)]
    in_maps = []
    for core in range(8):
        b, g = core // 4, core % 4
        cols = np.concatenate([
            np.arange(128 * g, 128 * g + 128),
            512 + np.arange(128 * g, 128 * g + 128),
            1536 + np.arange(64 * g, 64 * g + 64),
            2560 + np.arange(128 * g, 128 * g + 128),
            3072 + np.arange(16),
            1024 + np.arange(128 * g, 128 * g + 128),
            1792 + np.arange(64 * g, 64 * g + 64),
            2048 + np.arange(128 * g, 128 * g + 128),
        ])
        win_l = np.ascontiguousarray(w_in[:, cols].reshape(8, 128, 784).transpose(1, 0, 2))
        prm = np.zeros((128, PW), np.float32)
        prm[:, C_C:C_C + 8] = _pcol(c[b])
        prm[:, C_BADA:C_BADA + 48] = _pcol(b_ada)
        prm[:, C_G1:C_G1 + 8] = _pcol(g_norm1)
        prm[:, C_G2:C_G2 + 8] = _pcol(g_norm2)
        prm[:, C_GF:C_GF + 8] = _pcol(g_final)
        for t in range(3):
            prm[:, C_WC + 44 * t:C_WC + 44 * t + 44] = _pcol(w_conv[t])
        prm[:, C_BC:C_BC + 44] = _pcol(b_conv)
        prm[:, C_GGLA] = g_gla_out[128 * g:128 * g + 128]
        prm[:, C_HALO] = 0.0 if g == 0 else 1.0
        prm[0:16, C_WFG:C_WFG + 64] = w_fg2[:, 64 * g:64 * g + 64]
        prm[0, C_BFG:C_BFG + 64] = b_fg2[64 * g:64 * g + 64]
        for kk in range(8):
            prm[:, C_CONST + 128 * kk:C_CONST + 128 * kk + 128] = m8[kk]
        prm[:, C_MSUF:C_MSUF + 128] = msuf
        prm[:, C_CSEL:C_CSEL + 2] = csel
        prm[:, C_ONESF:C_ONESF + 128] = 1.0
        xB = np.zeros((DM, 2050), np.float32)
        lo = 2048 * g - 2
        if g == 0:
            xB[:, 2:] = xTs[b][:, 0:2048]
        else:
            xB[:] = xTs[b][:, lo:lo + 2050]
        in_maps.append({
            "xT": xTs[b], "xB": xB, "prm": prm, "wada": wada_l, "win": win_l, "wout": wout_l, "wup": wup_l,
            "wdn": wdn_l, "tokoff": np.array([[8 * g, 8 * max(g - 1, 0)]], np.int32),
        })
    res = run_bass_kernel_spmd(nc, in_maps, core_ids=list(range(8)))
    out = np.empty((2, SEQ, DM), np.float32)
    for core in range(8):
        b, g = core // 4, core % 4
        out[b, 2048 * g:2048 * g + 2048, :] = res.results[core]["yT"].T
    if _debug:
        return out, [res.results[core]["dbg_o"] for core in range(8)]
    return out
```
